# Optimizing a Trainium2 kernel written in Bass

```python
import math
import jax, jax.numpy as jnp
from jax import lax
import numpy as np

D_MODEL = 1024
BATCH = 32
SEQ = 2048
DEPTH = 4

GRID_W = 64
Q_BLOCK = 128
EPS = 1e-6
MIN_FORGET = 1e-20

A_HEADS = 6
A_KV_HEADS = 2
A_GROUP = A_HEADS // A_KV_HEADS
A_HEAD_DIM = 64
A_AXIS_DIM = A_HEAD_DIM // 2
ROPE_THETA = 10000.0
A_WIDTH = A_HEADS * A_HEAD_DIM
A_KV_WIDTH = A_KV_HEADS * A_HEAD_DIM

B_HEADS = 4
B_HEAD_DIM = 48
B_V_DIM = 2 * B_HEAD_DIM
B_QK_WIDTH = B_HEADS * 2 * B_HEAD_DIM
B_WIDTH = B_HEADS * B_V_DIM

C_HEADS = 4
C_KEY_DIM = 64
C_VAL_DIM = 64
C_KEY_WIDTH = C_HEADS * C_KEY_DIM
C_VAL_WIDTH = C_HEADS * C_VAL_DIM
C_CHUNK = 16

D_MIX = A_WIDTH + B_WIDTH + C_VAL_WIDTH

IN_SPLITS = (A_WIDTH, A_KV_WIDTH, A_KV_WIDTH,
             B_QK_WIDTH, B_QK_WIDTH, B_WIDTH,
             C_KEY_WIDTH, C_KEY_WIDTH, C_KEY_WIDTH, C_VAL_WIDTH, C_VAL_WIDTH)
IN_COLS = A_WIDTH + 2 * A_KV_WIDTH + 2 * B_QK_WIDTH + B_WIDTH + 3 * C_KEY_WIDTH + 2 * C_VAL_WIDTH

D_FF = 2816
CONV_W = 3

kernel_name = 'hymba_style_bidir_hybrid_trunk'


def rms_norm(x, gain):
    xf = x.astype(jnp.float32)
    y = xf * lax.rsqrt(jnp.mean(xf * xf, axis=-1, keepdims=True) + EPS)
    return (y * gain.astype(jnp.float32)).astype(x.dtype)


def split_columns(proj):
    parts, start = [], 0
    for w in IN_SPLITS:
        parts.append(proj[..., start:start + w])
        start += w
    return parts


def axial_rope_tables(seq):
    rows = seq // GRID_W
    row = jnp.repeat(jnp.arange(rows), GRID_W)
    col = jnp.tile(jnp.arange(GRID_W), rows)
    inv_freq = ROPE_THETA ** (-jnp.arange(0, A_AXIS_DIM, 2, dtype=jnp.float32) / A_AXIS_DIM)
    ang = jnp.stack([row, col], axis=-1).astype(jnp.float32)[..., None] * inv_freq
    return jnp.cos(ang), jnp.sin(ang)


def apply_axial_rope(x, cos, sin):
    bsz, seq, heads, d = x.shape
    xf = x.astype(jnp.float32).reshape(bsz, seq, heads, 2, A_AXIS_DIM)
    half = A_AXIS_DIM // 2
    x1, x2 = xf[..., :half], xf[..., half:]
    c, s = cos[None, :, None], sin[None, :, None]
    out = jnp.concatenate([x1 * c - x2 * s, x2 * c + x1 * s], axis=-1)
    return out.reshape(bsz, seq, heads, d).astype(x.dtype)


def alibi_slopes(n_heads):
    return jnp.asarray([2.0 ** (-8.0 * (h + 1) / n_heads) for h in range(n_heads)], jnp.float32)


def hgrn_lower_bounds(logits):
    p = jax.nn.softmax(logits.astype(jnp.float32), axis=0)
    return jnp.maximum(jnp.cumsum(p, axis=0) - p[0:1], 0.0)


def gqa_attention(q, k, v):
    bsz, seq, _, d = q.shape
    nb = seq // Q_BLOCK
    qb = q.reshape(bsz, nb, Q_BLOCK, A_KV_HEADS, A_GROUP, d).transpose(1, 0, 2, 3, 4, 5)
    scale = d ** -0.5

    def one(qblk):
        s = jnp.einsum('bqhgd,bkhd->bhgqk', qblk, k).astype(jnp.float32) * scale
        p = jax.nn.softmax(s, axis=-1).astype(v.dtype)
        return jnp.einsum('bhgqk,bkhd->bqhgd', p, v)

    o = lax.map(one, qb)
    return o.transpose(1, 0, 2, 3, 4, 5).reshape(bsz, seq, A_HEADS * d)


def diff_attention(q1, q2, k1, k2, v, lam, slopes):
    bsz, seq, heads, d = q1.shape
    nb = seq // Q_BLOCK
    scale = d ** -0.5
    pos = jnp.arange(seq)
    qpos = pos.reshape(nb, Q_BLOCK)

    def blocks(t):
        return t.reshape(bsz, nb, Q_BLOCK, heads, d).transpose(1, 0, 2, 3, 4)

    def one(args):
        q1b, q2b, qp = args
        dist = jnp.abs(qp[:, None] - pos[None, :]).astype(jnp.float32)
        bias = -slopes[:, None, None] * dist
        s1 = jnp.einsum('bqhd,bkhd->bhqk', q1b, k1).astype(jnp.float32) * scale + bias
        s2 = jnp.einsum('bqhd,bkhd->bhqk', q2b, k2).astype(jnp.float32) * scale + bias
        p = jax.nn.softmax(s1, axis=-1) - lam * jax.nn.softmax(s2, axis=-1)
        return jnp.einsum('bhqk,bkhv->bqhv', p.astype(v.dtype), v)

    o = lax.map(one, (blocks(q1), blocks(q2), qpos))
    return o.transpose(1, 0, 2, 3, 4).reshape(bsz, seq, heads, v.shape[-1])


def gated_linear_scan(q, k, v, log_f):
    q, k, v, log_f = (t.astype(jnp.float32) for t in (q, k, v, log_f))
    bsz, seq, heads, dk = q.shape
    dv = v.shape[-1]
    n = seq // C_CHUNK

    def chunks(t):
        return t.reshape(bsz, n, C_CHUNK, heads, t.shape[-1]).transpose(0, 3, 1, 2, 4)

    q, k, v, log_f = chunks(q), chunks(k), chunks(v), chunks(log_f)
    b = jnp.cumsum(log_f, axis=3)
    b_end = b[:, :, :, -1:, :]
    lower = jnp.tril(jnp.ones((C_CHUNK, C_CHUNK), dtype=bool))[:, :, None]
    diff = b[:, :, :, :, None, :] - b[:, :, :, None, :, :]
    decay = jnp.where(lower, jnp.exp(jnp.where(lower, diff, 0.0)), 0.0)
    intra = jnp.einsum('bhntd,bhnsd,bhntsd->bhnts', q, k, decay)
    o = jnp.einsum('bhnts,bhnsv->bhntv', intra, v)
    kv_chunk = jnp.einsum('bhnsd,bhnsv->nbhdv', k * jnp.exp(b_end - b), v)
    decay_chunk = jnp.exp(b_end[:, :, :, 0, :]).transpose(2, 0, 1, 3)

    def step(state, inp):
        dec, kv = inp
        return dec[..., None] * state + kv, state

    _, states = lax.scan(step, jnp.zeros((bsz, heads, dk, dv), jnp.float32), (decay_chunk, kv_chunk))
    o = o + jnp.einsum('bhntd,nbhdv->bhntv', q * jnp.exp(b), states)
    return o.transpose(0, 2, 3, 1, 4).reshape(bsz, seq, heads, dv)


def depthwise_conv(h, w, b):
    out = lax.conv_general_dilated(h, w[:, None, :].astype(h.dtype), window_strides=(1,), padding='SAME',
                                   dimension_numbers=('NWC', 'WIO', 'NWC'), feature_group_count=h.shape[-1])
    return out + b.astype(h.dtype)


def setup_inputs(seed: int = 0) -> dict:
    key = jax.random.key(seed)
    ks = jax.random.split(key, 17)
    f32 = jnp.float32

    def nrm(k, shape, scale):
        return jax.random.normal(k, shape, f32) * scale

    def gain(k, shape):
        return 1.0 + 0.02 * jax.random.normal(k, shape, f32)

    return {
        'x': nrm(ks[0], (BATCH, SEQ, D_MODEL), 1.0),
        'attn_norm': gain(ks[1], (DEPTH, D_MODEL)),
        'w_in': nrm(ks[2], (DEPTH, D_MODEL, IN_COLS), D_MODEL ** -0.5),
        'a_q_norm': gain(ks[3], (DEPTH, A_HEAD_DIM)),
        'a_k_norm': gain(ks[4], (DEPTH, A_HEAD_DIM)),
        'b_q_norm': gain(ks[5], (DEPTH, B_HEAD_DIM)),
        'b_k_norm': gain(ks[6], (DEPTH, B_HEAD_DIM)),
        'b_lambda': nrm(ks[7], (DEPTH, 4, B_HEAD_DIM), 0.1),
        'b_sub_norm': gain(ks[8], (DEPTH, B_WIDTH)),
        'c_lb_logits': nrm(ks[9], (DEPTH, C_KEY_WIDTH), 0.5),
        'c_out_norm': gain(ks[10], (DEPTH, C_VAL_WIDTH)),
        'w_out': nrm(ks[11], (DEPTH, D_MIX, D_MODEL), D_MIX ** -0.5),
        'ffn_norm': gain(ks[12], (DEPTH, D_MODEL)),
        'w_up': nrm(ks[13], (DEPTH, D_MODEL, 2 * D_FF), D_MODEL ** -0.5),
        'conv_w': nrm(ks[14], (DEPTH, CONV_W, D_FF), CONV_W ** -0.5),
        'conv_b': nrm(ks[15], (DEPTH, D_FF), 0.02),
        'w_down': nrm(ks[16], (DEPTH, D_FF, D_MODEL), D_FF ** -0.5),
    }


def reference(x, attn_norm, w_in, a_q_norm, a_k_norm, b_q_norm, b_k_norm, b_lambda, b_sub_norm,
              c_lb_logits, c_out_norm, w_out, ffn_norm, w_up, conv_w, conv_b, w_down):
    bsz, seq, _ = x.shape
    cos, sin = axial_rope_tables(seq)
    slopes = alibi_slopes(B_HEADS)
    lb_table = hgrn_lower_bounds(c_lb_logits)

    def rev(t):
        return jnp.flip(t, axis=1)

    for l in range(DEPTH):
        h = rms_norm(x, attn_norm[l])
        proj = h @ w_in[l].astype(h.dtype)
        a_q, a_k, a_v, b_q, b_k, b_v, c_q, c_ff, c_fb, c_v, c_g = split_columns(proj)

        aq = apply_axial_rope(rms_norm(a_q.reshape(bsz, seq, A_HEADS, A_HEAD_DIM), a_q_norm[l]), cos, sin)
        ak = apply_axial_rope(rms_norm(a_k.reshape(bsz, seq, A_KV_HEADS, A_HEAD_DIM), a_k_norm[l]), cos, sin)
        av = a_v.reshape(bsz, seq, A_KV_HEADS, A_HEAD_DIM)
        o_a = gqa_attention(aq, ak, av)

        bq = rms_norm(b_q.reshape(bsz, seq, B_HEADS, 2, B_HEAD_DIM), b_q_norm[l])
        bk = rms_norm(b_k.reshape(bsz, seq, B_HEADS, 2, B_HEAD_DIM), b_k_norm[l])
        bv = b_v.reshape(bsz, seq, B_HEADS, B_V_DIM)
        lam_init = 0.8 - 0.6 * math.exp(-0.3 * l)
        lam_p = b_lambda[l].astype(jnp.float32)
        lam = jnp.exp(jnp.sum(lam_p[0] * lam_p[1])) - jnp.exp(jnp.sum(lam_p[2] * lam_p[3])) + lam_init
        o_b = diff_attention(bq[..., 0, :], bq[..., 1, :], bk[..., 0, :], bk[..., 1, :], bv, lam, slopes)
        o_b = (rms_norm(o_b, b_sub_norm[l].reshape(B_HEADS, B_V_DIM)) * (1.0 - lam_init)).reshape(bsz, seq, B_WIDTH)

        lb = lb_table[l]

        def log_forget(z):
            f = lb + (1.0 - lb) * jax.nn.sigmoid(z.astype(jnp.float32))
            return jnp.log(jnp.maximum(f, MIN_FORGET)).reshape(bsz, seq, C_HEADS, C_KEY_DIM)

        cq = jax.nn.silu(c_q).reshape(bsz, seq, C_HEADS, C_KEY_DIM)
        cv = c_v.reshape(bsz, seq, C_HEADS, C_VAL_DIM)
        lf_fwd = log_forget(c_ff)
        lf_bwd = log_forget(c_fb)
        o_fwd = gated_linear_scan(cq, -jnp.expm1(lf_fwd), cv, lf_fwd)
        o_bwd = rev(gated_linear_scan(rev(cq), rev(-jnp.expm1(lf_bwd)), rev(cv), rev(lf_bwd)))
        o_c = rms_norm(o_fwd + o_bwd, c_out_norm[l].reshape(C_HEADS, C_VAL_DIM)).astype(x.dtype)
        o_c = (o_c * jax.nn.silu(c_g).reshape(bsz, seq, C_HEADS, C_VAL_DIM)).reshape(bsz, seq, C_VAL_WIDTH)

        mix = jnp.concatenate([o_a, o_b.astype(x.dtype), o_c], axis=-1)
        x = x + mix @ w_out[l].astype(x.dtype)

        hf = rms_norm(x, ffn_norm[l]) @ w_up[l].astype(x.dtype)
        gate, val = hf[..., :D_FF], hf[..., D_FF:]
        gate = depthwise_conv(gate, conv_w[l], conv_b[l])
        x = x + (jax.nn.gelu(gate, approximate=False) * val) @ w_down[l].astype(x.dtype)
    return x
```

```python
import math
import numpy as np
import ml_dtypes
import concourse.bass as bass
import concourse.mybir as mybir
from concourse.bass_utils import run_bass_kernel_spmd

F32 = mybir.dt.float32
BF16 = mybir.dt.bfloat16
ALU = mybir.AluOpType
AF = mybir.ActivationFunctionType
AX = mybir.AxisListType

D = 1024
S = 2048
DEPTH = 4
NCORES = 8
NSEQ = 4
EPS = 1e-6
DFF = 2816
NJ = DFF // 128
SLOPES = [2.0 ** (-8.0 * (h + 1) / 4) for h in range(4)]
LAM_INIT = [0.8 - 0.6 * math.exp(-0.3 * l) for l in range(DEPTH)]
ENG = ("pe", "act", "dve", "pool", "sp")


class Buf:
    __slots__ = ("name", "w", "rd")

    def __init__(self, name):
        self.name = name
        self.w = None
        self.rd = {}


def mkbufs(name, *dims):
    if not dims:
        return Buf(name)
    return [mkbufs(f"{name}_{i}", *dims[1:]) for i in range(dims[0])]


class Slot:
    def __init__(self, sem, key):
        self.sem = sem
        self.key = key
        self.cnt = 0


class Prog:
    def __init__(self, nc, es):
        self.nc = nc
        self.e = {"pe": nc.tensor, "act": nc.scalar, "dve": nc.vector, "pool": nc.gpsimd, "sp": nc.sync}
        self.sem = {k: es.enter_context(nc.semaphore("sem_" + k)) for k in ENG}
        self.cnt = {k: 0 for k in ENG}
        self.seen = {k: {} for k in ENG}
        self.es = es
        self.slots = []
        self.nwait = 0
        self.nop = 0

    def slot(self, name):
        s = Slot(self.es.enter_context(self.nc.semaphore("dsem_" + name)), "slot_" + name + str(len(self.slots)))
        self.slots.append(s)
        return s

    def _wait(self, e, tok):
        key, val = tok
        if self.seen[e].get(key, 0) >= val:
            return
        sem = self.sem[key] if key in self.sem else self._slot_by_key[key].sem
        self.e[e].wait_ge(sem, val)
        self.seen[e][key] = val
        self.nwait += 1

    @property
    def _slot_by_key(self):
        return {s.key: s for s in self.slots}

    def _deps(self, e, reads, writes):
        toks = {}

        def add(t):
            if t is None:
                return
            k, v = t
            if toks.get(k, 0) < v:
                toks[k] = v

        for b in reads:
            add(b.w)
        for b in writes:
            add(b.w)
            for k, v in b.rd.items():
                add((k, v))
        for k, v in toks.items():
            if e == "pe" and k == "pe":
                continue
            self._wait(e, (k, v))

    def op(self, e, fn, reads=(), writes=(), pe_sync=False):
        self._deps(e, reads, writes)
        if pe_sync and e == "pe" and self.cnt["pe"] > 0:
            self._wait("pe", ("pe", self.cnt["pe"]))
        inst = fn(self.e[e])
        self.cnt[e] += 1
        inst.then_inc(self.sem[e], 1)
        tok = (e, self.cnt[e])
        for b in writes:
            b.w = tok
            b.rd = {}
        for b in reads:
            if b.rd.get(e, 0) < tok[1]:
                b.rd[e] = tok[1]
        self.nop += 1
        return inst

    def dma(self, q, slot, out, in_, reads=(), writes=()):
        self._deps(q, reads, writes)
        inst = self.e[q].dma_start(out=out, in_=in_)
        slot.cnt += 16
        inst.then_inc(slot.sem, 16)
        tok = (slot.key, slot.cnt)
        for b in writes:
            b.w = tok
            b.rd = {}
        for b in reads:
            b.rd[slot.key] = tok[1]
        self.nop += 1

    def dma_group(self, q, slot, pairs, reads=(), writes=()):
        self._deps(q, reads, writes)
        for out, in_ in pairs:
            inst = self.e[q].dma_start(out=out, in_=in_)
            slot.cnt += 16
            inst.then_inc(slot.sem, 16)
            self.nop += 1
        tok = (slot.key, slot.cnt)
        for b in writes:
            b.w = tok
            b.rd = {}
        for b in reads:
            b.rd[slot.key] = tok[1]

    def barrier(self):
        sk = self._slot_by_key
        for e in ENG:
            for f in ENG:
                if f != e and self.cnt[f] > 0:
                    self._wait(e, (f, self.cnt[f]))
            for s in self.slots:
                if s.cnt > 0:
                    self._wait(e, (s.key, s.cnt))


def bcast(ap, axis, shape):
    return ap.unsqueeze(axis).to_broadcast(list(shape))


def build_program(nseq=NSEQ, depth=DEPTH, stages=("C", "A", "B", "F"), debug_out=None):
    from contextlib import ExitStack

    nc = bass.Bass("TRN2", target_bir_lowering=False)
    dt = nc.dram_tensor
    xT_d = dt("xT", [nseq, 8, 128, S], F32, kind="ExternalInput").ap()
    out_d = dt("outT", [nseq, 8, 128, S], F32, kind="ExternalOutput").ap()
    wA_d = dt("wA", [DEPTH, 2, 128, 8, 320], F32, kind="ExternalInput").ap()
    wB_d = dt("wB", [DEPTH, 4, 128, 8, 288], F32, kind="ExternalInput").ap()
    wC_d = dt("wC", [DEPTH, 2, 128, 8, 640], F32, kind="ExternalInput").ap()
    wO_d = dt("wO", [DEPTH, 9, 128, 1024], F32, kind="ExternalInput").ap()
    wU_d = dt("wU", [DEPTH, NJ, 128, 8, 256], F32, kind="ExternalInput").ap()
    wD_d = dt("wD", [DEPTH, 8, 128, NJ, 128], F32, kind="ExternalInput").ap()
    pF_d = dt("pF", [128, PF_COLS], F32, kind="ExternalInput").ap()
    cB_d = dt("cB", [128, CB_COLS], F32, kind="ExternalInput").ap()

    with ExitStack() as es:
        P = Prog(nc, es)

        uid = [0]

        def sb(name, shape, dtype, scope=es):
            uid[0] += 1
            return scope.enter_context(nc.sbuf_tensor(f"{name}_s{uid[0]}", list(shape), dtype))

        X = sb("X", [128, 8, S], F32)
        Xb = mkbufs("X", 8, 4)
        pF = sb("pF", [128, PF_COLS], F32)
        cB = sb("cB", [128, CB_COLS], BF16)
        pF_b, cB_b = Buf("pF"), Buf("cB")
        PSALL = es.enter_context(nc.psum_tensor("psall", [128, 4096], F32))
        PS = [PSALL[:, 512 * i:512 * (i + 1)] for i in range(8)]
        PSb = mkbufs("ps", 8)
        der = sb("der", [128, 64], F32)
        der_b = Buf("der")

        def pf(name, *idx):
            off, shp = PF_OFF[name]
            ap = pF[:, off:off + int(np.prod(shp))]
            if len(shp) > 1:
                names = " ".join(f"a{i}" for i in range(len(shp)))
                ap = ap.rearrange(f"p ({names}) -> p {names}", **{f"a{i}": shp[i] for i in range(len(shp))})
            return ap

        def cb(name):
            off, shp = CB_OFF[name]
            ap = cB[:, off:off + int(np.prod(shp))]
            if len(shp) > 1:
                names = " ".join(f"a{i}" for i in range(len(shp)))
                ap = ap.rearrange(f"p ({names}) -> p {names}", **{f"a{i}": shp[i] for i in range(len(shp))})
            return ap

        s_const = P.slot("const")
        s_const2 = P.slot("const2")
        P.dma("sp", s_const, pF[:], pF_d, writes=[pF_b])
        P.dma_group("pool", s_const2, [(cB[:, i:min(CB_COLS, i + 2048)], cB_d[:, i:min(CB_COLS, i + 2048)]) for i in range(0, CB_COLS, 2048)], writes=[cB_b])

        ident = cb("ident")
        ones = cb("ones")
        blk1 = cb("blk1")

        lamv = pf("blam")
        tmpd = sb("tmpd", [128, 2 * DEPTH * 48 + 64], F32)
        tmpd_b = Buf("tmpd")
        pr = tmpd[:, 0:2 * DEPTH * 48].rearrange("p (t l k) -> p t l k", t=2, l=DEPTH)
        P.op("dve", lambda e: e.tensor_tensor(out=pr[:, 0], in0=lamv[:, :, 0, :], in1=lamv[:, :, 1, :], op=ALU.mult), [pF_b], [tmpd_b])
        P.op("dve", lambda e: e.tensor_tensor(out=pr[:, 1], in0=lamv[:, :, 2, :], in1=lamv[:, :, 3, :], op=ALU.mult), [pF_b, tmpd_b], [tmpd_b])
        sc = tmpd[:, 2 * DEPTH * 48:2 * DEPTH * 48 + 64]
        P.op("dve", lambda e: e.tensor_reduce(out=sc[:, 0:2 * DEPTH], in_=tmpd[:, 0:2 * DEPTH * 48].rearrange("p (a k) -> p a k", k=48), axis=AX.X, op=ALU.add), [tmpd_b], [tmpd_b])
        P.op("act", lambda e: e.activation(out=sc[:, 8:8 + 2 * DEPTH], in_=sc[:, 0:2 * DEPTH], func=AF.Exp), [tmpd_b], [tmpd_b])
        P.op("dve", lambda e: e.tensor_tensor(out=der[:, 0:DEPTH], in0=sc[:, 8:8 + DEPTH], in1=sc[:, 8 + DEPTH:8 + 2 * DEPTH], op=ALU.subtract), [tmpd_b], [der_b])
        for l in range(DEPTH):
            P.op("dve", lambda e, l=l: e.tensor_scalar(out=der[:, l:l + 1], in0=der[:, l:l + 1], scalar1=float(LAM_INIT[l]), scalar2=None, op0=ALU.add), [der_b], [der_b])
        P.op("dve", lambda e: e.tensor_scalar(out=der[:, 4:8], in0=der[:, 0:4], scalar1=-1.0, scalar2=None, op0=ALU.mult), [der_b], [der_b])
        lg = pf("clb")
        w0 = sc[:, 16:24].rearrange("p (a l) -> p a l", a=2)
        w1 = sc[:, 24:32].rearrange("p (a l) -> p a l", a=2)
        mx = sc[:, 32:34]
        P.op("dve", lambda e: e.tensor_reduce(out=mx, in_=lg, axis=AX.X, op=ALU.max), [pF_b, tmpd_b], [tmpd_b])
        P.op("dve", lambda e: e.tensor_tensor(out=w0, in0=lg, in1=bcast(mx, 2, [128, 2, DEPTH]), op=ALU.subtract), [pF_b, tmpd_b], [tmpd_b])
        P.op("act", lambda e: e.activation(out=w1, in_=w0, func=AF.Exp), [tmpd_b], [tmpd_b])
        sm = sc[:, 34:36]
        P.op("dve", lambda e: e.tensor_reduce(out=sm, in_=w1, axis=AX.X, op=ALU.add), [tmpd_b], [tmpd_b])
        P.op("dve", lambda e: e.reciprocal(out=sm, in_=sm), [tmpd_b], [tmpd_b])
        P.op("dve", lambda e: e.tensor_tensor(out=w1, in0=w1, in1=bcast(sm, 2, [128, 2, DEPTH]), op=ALU.mult), [tmpd_b], [tmpd_b])
        lbv = der[:, 8:16].rearrange("p (a l) -> p a l", a=2)
        omv = der[:, 16:24].rearrange("p (a l) -> p a l", a=2)
        P.op("dve", lambda e: e.memset(lbv[:, :, 0:1], 0.0), [der_b], [der_b])
        for l in range(1, DEPTH):
            P.op("dve", lambda e, l=l: e.tensor_tensor(out=lbv[:, :, l:l + 1], in0=lbv[:, :, l - 1:l], in1=w1[:, :, l:l + 1], op=ALU.add), [tmpd_b, der_b], [der_b])
        P.op("dve", lambda e: e.tensor_scalar(out=lbv, in0=lbv, scalar1=0.0, scalar2=None, op0=ALU.max), [der_b], [der_b])
        P.op("dve", lambda e: e.tensor_scalar(out=omv, in0=lbv, scalar1=-1.0, scalar2=1.0, op0=ALU.mult, op1=ALU.add), [der_b], [der_b])
        gsub = der[:, 24:24 + 4 * DEPTH].rearrange("p (l h) -> p l h", l=DEPTH)
        for l in range(DEPTH):
            P.op("dve", lambda e, l=l: e.tensor_scalar(out=gsub[:, l, :], in0=pf("bsub")[:, l, :], scalar1=float(1.0 - LAM_INIT[l]), scalar2=None, op0=ALU.mult), [pF_b, der_b], [der_b])

        s_x = P.slot("x")
        s_o = P.slot("o")
        wslots = {k: [P.slot(f"{k}{i}") for i in range(n)] for k, n in (("wA", 1), ("wB", 1), ("wC", 1), ("wO", 1), ("wU", 2), ("wD", 2))}

        def rmsnorm_T(gname, l, hT, hTb, scope):
            sq = sb("n_sq", [128, 3, 512], BF16, scope)
            sqb = mkbufs("n_sq", 3)
            tt = sb("n_t", [128, 2, 512], F32, scope)
            ttb = mkbufs("n_t", 2)
            g = pf(gname)
            for tb in range(4):
                cs = slice(tb * 512, (tb + 1) * 512)
                pb = 6 + (tb % 2)
                for c in range(8):
                    r = c % 3
                    P.op("act", lambda e, c=c, r=r: e.activation(out=sq[:, r, :], in_=X[:, c, cs], func=AF.Square), [Xb[c][tb]], [sqb[r]])
                    P.op("pe", lambda e, c=c, r=r: e.matmul(PS[pb][:, :], lhsT=ones, rhs=sq[:, r, :], start=(c == 0), stop=(c == 7)), [sqb[r], cB_b], [PSb[pb]])
                k = tb % 2
                P.op("act", lambda e: e.activation(out=tt[:, k, :], in_=PS[pb][:, :], func=AF.Ln, scale=1.0 / D, bias=cb_eps(1)), [PSb[pb], pF_b], [ttb[k]])
                P.op("act", lambda e: e.activation(out=tt[:, k, :], in_=tt[:, k, :], func=AF.Exp, scale=-0.5), [ttb[k]], [ttb[k]])
                for c in range(8):
                    P.op("dve", lambda e, c=c: e.scalar_tensor_tensor(out=hT[:, c, cs], in0=X[:, c, cs], scalar=g[:, l, c:c + 1], in1=tt[:, k, :], op0=ALU.mult, op1=ALU.mult),
                         [Xb[c][tb], ttb[k], pF_b], [hTb[c][tb]])

        def cb_eps(mult):
            idx = {1: 0, 64: 1, 48: 2}[mult]
            return pf("epsv")[:, idx:idx + 1]

        def load_w(slot, dst, dst_b, src, kcs):
            n = src.shape[-1]
            step = max(1, 2048 // n)
            P.dma_group("pool", slot, [(dst[:, k0:min(kcs, k0 + step), :], src[:, k0:min(kcs, k0 + step), :]) for k0 in range(0, kcs, step)], writes=dst_b)

        def job_A(l, g, hT, hTb, mix, mixb, scope):
            W = sb("a_w", [128, 8, 320], BF16, scope); Wb = mkbufs("a_w", 3)
            load_w(wslots["wA"][0], W, Wb, wA_d[l, g], 8)
            QT = sb("a_qt", [128, 4, S], BF16, scope); QTb = mkbufs("a_qt", 16)
            Vp = sb("a_vp", [128, 16, 128], BF16, scope); Vpb = mkbufs("a_vp", 16)
            Vp1 = Buf("a_vp1")
            P.op("pool", lambda e: e.memset(Vp[:, :, 64:128], 1.0), [], [Vp1])
            NR = 3
            from contextlib import ExitStack as _ES
            sci = scope.enter_context(_ES())
            sq = sb("a_sq", [128, NR, 256], F32, sci); sqb = mkbufs("a_sq", NR)
            st = sb("a_st", [128, NR, 16], F32, sci); stb = mkbufs("a_st", NR)
            t1 = sb("a_t1", [128, NR, 256], F32, sci); t1b = mkbufs("a_t1", NR)
            t2 = sb("a_t2", [128, NR, 256], F32, sci); t2b = mkbufs("a_t2", NR)
            rp = sb("a_rp", [128, NR, 4, 128], F32, sci); rpb = mkbufs("a_rp", NR)
            rot = sb("a_rot", [128, NR, 448], BF16, sci); rotb = mkbufs("a_rot", NR)
            P.op("pool", lambda e: e.memset(rot[:, :, 256:384], 0.0), [], rotb)
            gA = pf("gA")
            cs_ = pf("cos")
            sn_ = pf("sin")

            def emit_P(i):
                pb = 3 + i % NR
                ts = slice(i * 128, (i + 1) * 128)
                for c in range(8):
                    P.op("pe", lambda e, c=c: e.matmul(PS[pb][:, 0:320], lhsT=hT[:, c, ts], rhs=W[:, c, :], start=(c == 0), stop=(c == 7)),
                         [hTb[c][i // 4]] + Wb, [PSb[pb]])

            def emit_E(i):
                r = i % NR
                pb = 3 + i % NR
                P.op("act", lambda e: e.activation(out=sq[:, r, :], in_=PS[pb][:, 0:256], func=AF.Square), [PSb[pb]], [sqb[r]])
                P.op("act", lambda e: e.activation(out=Vp[:, i, 0:64], in_=PS[pb][:, 256:320], func=AF.Copy), [PSb[pb]], [Vpb[i]])
                ss = st[:, r, 0:4]; tq = st[:, r, 4:8]
                P.op("dve", lambda e: e.tensor_reduce(out=ss, in_=sq[:, r, :].rearrange("p (h d) -> p h d", h=4), axis=AX.X, op=ALU.add), [sqb[r]], [stb[r]])
                P.op("act", lambda e: e.activation(out=tq[:, 0:3], in_=ss[:, 0:3], func=AF.Ln, scale=1.0, bias=cb_eps(64)), [stb[r], pF_b], [stb[r]])
                P.op("act", lambda e: e.activation(out=tq[:, 3:4], in_=ss[:, 3:4], func=AF.Ln, scale=1.0 / 64, bias=cb_eps(1)), [stb[r], pF_b], [stb[r]])
                P.op("act", lambda e: e.activation(out=tq, in_=tq, func=AF.Exp, scale=-0.5), [stb[r]], [stb[r]])
                P.op("dve", lambda e: e.tensor_tensor(out=t1[:, r, :].rearrange("p (h d) -> p h d", h=4), in0=PS[pb][:, 0:256].rearrange("p (h d) -> p h d", h=4),
                                                      in1=bcast(tq, 2, [128, 4, 64]), op=ALU.mult), [PSb[pb], stb[r]], [t1b[r]])
                P.op("pool", lambda e: e.tensor_tensor(out=t2[:, r, :], in0=t1[:, r, :], in1=gA[:, l, :], op=ALU.mult), [t1b[r], pF_b], [t2b[r]])
                v = t2[:, r, :].rearrange("p (g h j) -> p g h j", g=8, h=2)
                x1, x2 = v[:, :, 0, :], v[:, :, 1, :]
                cosb = bcast(cs_[:, i, :].rearrange("p (a j) -> p a j", a=2), 1, [128, 4, 2, 16])
                sinb = bcast(sn_[:, i, :].rearrange("p (a j) -> p a j", a=2), 1, [128, 4, 2, 16])
                x1v = x1.rearrange("p (h a) j -> p h a j", h=4)
                x2v = x2.rearrange("p (h a) j -> p h a j", h=4)
                pa = [rp[:, r, k, :].rearrange("p (h a j) -> p h a j", h=4, a=2) for k in range(4)]
                P.op("pool", lambda e: e.tensor_tensor(out=pa[0], in0=x1v, in1=cosb, op=ALU.mult), [t2b[r], pF_b], [rpb[r]])
                P.op("dve", lambda e: e.tensor_tensor(out=pa[1], in0=x2v, in1=sinb, op=ALU.mult), [t2b[r], pF_b], [rpb[r]])
                P.op("pool", lambda e: e.tensor_tensor(out=pa[2], in0=x2v, in1=cosb, op=ALU.mult), [t2b[r], pF_b], [rpb[r]])
                P.op("dve", lambda e: e.tensor_tensor(out=pa[3], in0=x1v, in1=sinb, op=ALU.mult), [t2b[r], pF_b], [rpb[r]])
                ro = rot[:, r, 0:256].rearrange("p (g h j) -> p g h j", g=8, h=2)
                P.op("dve", lambda e: e.tensor_tensor(out=ro[:, :, 0, :], in0=rp[:, r, 0, :].rearrange("p (g j) -> p g j", g=8), in1=rp[:, r, 1, :].rearrange("p (g j) -> p g j", g=8), op=ALU.subtract),
                     [rpb[r]], [rotb[r]])
                P.op("pool", lambda e: e.tensor_tensor(out=ro[:, :, 1, :], in0=rp[:, r, 2, :].rearrange("p (g j) -> p g j", g=8), in1=rp[:, r, 3, :].rearrange("p (g j) -> p g j", g=8), op=ALU.add),
                     [rpb[r]], [rotb[r]])
                P.op("pool", lambda e: e.tensor_copy(out=rot[:, r, 384:448], in_=rot[:, r, 192:256]), [rotb[r]], [rotb[r]])

            def emit_T(i):
                r = i % NR
                tk = 6 + i % 2
                ts = slice(i * 128, (i + 1) * 128)
                tp = PS[tk][:, :].bitcast(BF16)
                for k, c0 in enumerate((0, 128, 192, 320)):
                    P.op("pe", lambda e, k=k, c0=c0: e.transpose(tp[:, k * 128:(k + 1) * 128], rot[:, r, c0:c0 + 128], ident), [rotb[r], cB_b], [PSb[tk]])
                P.op("act", lambda e: e.activation(out=QT[:, :, ts], in_=tp[:, 0:512].rearrange("p (k t) -> p k t", k=4), func=AF.Copy), [PSb[tk]], [QTb[i]])

            emit_P(0)
            emit_P(1)
            emit_E(0)
            for i in range(16):
                if i + 2 < 16:
                    emit_P(i + 2)
                if i + 1 < 16:
                    emit_E(i + 1)
                emit_T(i)
            P.barrier()
            sci.close()
            PT = sb("a_pt", [128, 3, 1024], BF16, scope); PTb = mkbufs("a_pt", 3)
            rec = sb("a_rec", [128, 2, 512], F32, scope); recb = mkbufs("a_rec", 2)
            pairs = [(hq, qb, jp) for hq in range(3) for qb in range(4) for jp in range(8)]
            hq_loc = [(0, 2), (0, 3), (1, 2)]

            def emit_S(n):
                hq, qb, jp = pairs[n]
                ti, kt = hq_loc[hq]
                pr3 = n % 3
                for u in range(2):
                    j = 2 * jp + u
                    bk = 2 * pr3 + u
                    P.op("pe", lambda e, j=j, bk=bk: e.matmul(PS[bk][:, :], lhsT=QT[:, kt, j * 128:(j + 1) * 128], rhs=QT[:, ti, qb * 512:(qb + 1) * 512], start=True, stop=True),
                         [QTb[j]] + [QTb[qb * 4 + k] for k in range(4)], [PSb[bk]])
                P.op("act", lambda e: e.activation(out=PT[:, pr3, :], in_=PSALL[:, 1024 * pr3:1024 * (pr3 + 1)], func=AF.Exp), [PSb[2 * pr3], PSb[2 * pr3 + 1]], [PTb[pr3]])

            emit_S(0)
            for n, (hq, qb, jp) in enumerate(pairs):
                if n + 1 < len(pairs):
                    emit_S(n + 1)
                ob = 6 + (hq * 4 + qb) % 2
                pr3 = n % 3
                for u in range(2):
                    j = 2 * jp + u
                    P.op("pe", lambda e, j=j, u=u: e.matmul(PS[ob][:, :], lhsT=Vp[:, j, :], rhs=PT[:, pr3, u * 512:(u + 1) * 512], start=(j == 0), stop=(j == 15)), [Vpb[j], Vp1, PTb[pr3]], [PSb[ob]])
                if jp == 7:
                    H = 3 * g + hq
                    rr = (hq * 4 + qb) % 2
                    po = (H % 2) * 64
                    P.op("dve", lambda e: e.reciprocal(out=rec[64:128, rr, :], in_=PS[ob][64:128, :]), [PSb[ob]], [recb[rr]])
                    P.op("dve", lambda e: e.tensor_tensor(out=mix(H // 2)[po:po + 64, qb * 512:(qb + 1) * 512], in0=PS[ob][0:64, :], in1=rec[64:128, rr, :], op=ALU.mult),
                         [PSb[ob], recb[rr]], [mixb[H // 2][qb][H % 2]])

        def job_B(l, h, hT, hTb, mix, mixb, scope):
            W = sb("b_w", [128, 8, 288], BF16, scope); Wb = mkbufs("b_w", 3)
            load_w(wslots["wB"][0], W, Wb, wB_d[l, h], 8)
            QB = sb("b_qb", [128, 6, S], BF16, scope); QBb = mkbufs("b_qb", 16)
            P.op("pool", lambda e: e.memset(QB[64:128, 2:4, :], 0.0), [], QBb)
            P.op("pool", lambda e: e.memset(QB[0:64, 4:6, :], 0.0), [], QBb)
            Vp = sb("b_vp", [128, 16, 128], BF16, scope); Vpb = mkbufs("b_vp", 16)
            Vp1 = Buf("b_vp1")
            P.op("pool", lambda e: e.memset(Vp[:, :, 96:128], 1.0), [], [Vp1])
            from contextlib import ExitStack as _ES
            sci = scope.enter_context(_ES())
            NR = 3
            sq = sb("b_sq", [128, NR, 192], F32, sci); sqb = mkbufs("b_sq", NR)
            st = sb("b_st", [128, NR, 16], F32, sci); stb = mkbufs("b_st", NR)
            t1 = sb("b_t1", [128, NR, 192], F32, sci); t1b = mkbufs("b_t1", NR)
            stg = sb("b_stg", [128, NR, 512], BF16, sci); stgb = mkbufs("b_stg", NR)
            P.op("pool", lambda e: e.memset(stg[:, :, :], 0.0), [], stgb)
            gB = pf("gB")
            aug = cb("aug")

            def emit_P(i):
                pb = 3 + i % NR
                ts = slice(i * 128, (i + 1) * 128)
                for c in range(8):
                    P.op("pe", lambda e, c=c: e.matmul(PS[pb][:, 0:288], lhsT=hT[:, c, ts], rhs=W[:, c, :], start=(c == 0), stop=(c == 7)),
                         [hTb[c][i // 4]] + Wb, [PSb[pb]])

            def emit_E(i):
                r = i % NR
                pb = 3 + i % NR
                P.op("act", lambda e: e.activation(out=sq[:, r, :], in_=PS[pb][:, 0:192], func=AF.Square), [PSb[pb]], [sqb[r]])
                P.op("act", lambda e: e.activation(out=Vp[:, i, 0:96], in_=PS[pb][:, 192:288], func=AF.Copy), [PSb[pb]], [Vpb[i]])
                ss = st[:, r, 0:4]; tq = st[:, r, 4:8]
                P.op("dve", lambda e: e.tensor_reduce(out=ss, in_=sq[:, r, :].rearrange("p (h d) -> p h d", h=4), axis=AX.X, op=ALU.add), [sqb[r]], [stb[r]])
                P.op("act", lambda e: e.activation(out=tq[:, 0:2], in_=ss[:, 0:2], func=AF.Ln, scale=1.0, bias=cb_eps(48)), [stb[r], pF_b], [stb[r]])
                P.op("act", lambda e: e.activation(out=tq[:, 2:4], in_=ss[:, 2:4], func=AF.Ln, scale=1.0 / 48, bias=cb_eps(1)), [stb[r], pF_b], [stb[r]])
                P.op("act", lambda e: e.activation(out=tq, in_=tq, func=AF.Exp, scale=-0.5), [stb[r]], [stb[r]])
                P.op("dve", lambda e: e.tensor_tensor(out=t1[:, r, :].rearrange("p (h d) -> p h d", h=4), in0=PS[pb][:, 0:192].rearrange("p (h d) -> p h d", h=4),
                                                      in1=bcast(tq, 2, [128, 4, 48]), op=ALU.mult), [PSb[pb], stb[r]], [t1b[r]])
                sv = stg[:, r, :].rearrange("p (g u c) -> p g u c", g=4, u=2)
                P.op("pool", lambda e: e.tensor_tensor(out=sv[:, :, 0, 0:48], in0=t1[:, r, :].rearrange("p (h d) -> p h d", h=4),
                                                       in1=gB[:, l, :].rearrange("p (h d) -> p h d", h=4), op=ALU.mult), [t1b[r], pF_b], [stgb[r]])
                P.op("dve", lambda e: e.tensor_tensor(out=sv[:, :, 1, 0:48], in0=t1[:, r, :].rearrange("p (h d) -> p h d", h=4),
                                                      in1=gB[:, l, :].rearrange("p (h d) -> p h d", h=4), op=ALU.mult), [t1b[r], pF_b], [stgb[r]])
                P.op("pool", lambda e: e.tensor_copy(out=sv[:, :, :, 48:52], in_=aug[:, h, i, :].rearrange("p (g u c) -> p g u c", g=4, u=2)), [cB_b], [stgb[r]])

            def emit_T(i):
                r = i % NR
                tk = 6 + i % 2
                ts = slice(i * 128, (i + 1) * 128)
                tp = PS[tk][:, :].bitcast(BF16)
                for k in range(4):
                    P.op("pe", lambda e, k=k: e.transpose(tp[:, k * 128:(k + 1) * 128], stg[:, r, k * 128:(k + 1) * 128], ident), [stgb[r], cB_b], [PSb[tk]])
                P.op("act", lambda e: e.activation(out=QB[:, 0:2, ts], in_=tp[:, 0:256].rearrange("p (k t) -> p k t", k=2), func=AF.Copy), [PSb[tk]], [QBb[i]])
                P.op("dve", lambda e: e.tensor_copy(out=QB[0:64, 2:4, ts], in_=tp[0:64, 256:512].rearrange("p (k t) -> p k t", k=2)), [PSb[tk]], [QBb[i]])
                P.op("act", lambda e: e.activation(out=QB[64:128, 4:6, ts], in_=tp[64:128, 256:512].rearrange("p (k t) -> p k t", k=2), func=AF.Copy), [PSb[tk]], [QBb[i]])

            emit_P(0)
            emit_P(1)
            emit_E(0)
            for i in range(16):
                if i + 2 < 16:
                    emit_P(i + 2)
                if i + 1 < 16:
                    emit_E(i + 1)
                emit_T(i)
            P.barrier()
            sci.close()
            PT = sb("b_pt", [128, 3, 1024], BF16, scope); PTb = mkbufs("b_pt", 3)
            rc = sb("b_rc", [128, 1, 512], F32, scope); rcb = mkbufs("b_rc", 1) * 2
            vv = sb("b_vv", [128, 2, 512], F32, scope); vvb = mkbufs("b_vv", 2)
            sqv = sb("b_sqv", [128, 512], BF16, scope); sqvb = Buf("b_sqv")
            tv = sb("b_tv", [128, 512], F32, scope); tvb = Buf("b_tv")
            dg = cb("dg")
            pairs = [(qb, cp, jp) for qb in range(4) for cp in range(2) for jp in range(8)]

            def emit_S(n):
                qb, cp, jp = pairs[n]
                qg, kp, km = cp, 2 + cp, 4 + cp
                pr3 = n % 3
                q0 = qb * 512
                for u in range(2):
                    j = 2 * jp + u
                    bk = 2 * pr3 + u
                    rel = j - 4 * qb
                    kc = slice(j * 128, (j + 1) * 128)
                    rd = [QBb[j]] + [QBb[qb * 4 + k] for k in range(4)]
                    if rel < 0:
                        P.op("pe", lambda e: e.matmul(PS[bk][:, :], lhsT=QB[:, kp, kc], rhs=QB[:, qg, q0:q0 + 512], start=True, stop=True), rd, [PSb[bk]])
                    elif rel > 3:
                        P.op("pe", lambda e: e.matmul(PS[bk][:, :], lhsT=QB[:, km, kc], rhs=QB[:, qg, q0:q0 + 512], start=True, stop=True), rd, [PSb[bk]])
                    else:
                        m = rel
                        mms = []
                        if m > 0:
                            mms.append((PS[bk][:, 0:128 * m], QB[:, km, kc], QB[:, qg, q0:q0 + 128 * m], rd))
                        mms.append((PS[bk][:, 128 * m:512], QB[:, kp, kc], QB[:, qg, q0 + 128 * m:q0 + 512], rd))
                        mms.append((PS[bk][:, 128 * m:128 * m + 128], ident, dg[:, h, :], [cB_b]))
                        for k_, (o_, l_, r_, rd_) in enumerate(mms):
                            P.op("pe", lambda e, o_=o_, l_=l_, r_=r_, k_=k_: e.matmul(o_, lhsT=l_, rhs=r_, start=(k_ == 0), stop=(k_ == len(mms) - 1)), rd_, [PSb[bk]])
                P.op("act", lambda e: e.activation(out=PT[:, pr3, :], in_=PSALL[:, 1024 * pr3:1024 * (pr3 + 1)], func=AF.Exp), [PSb[2 * pr3], PSb[2 * pr3 + 1]], [PTb[pr3]])

            emit_S(0)
            for n, (qb, cp, jp) in enumerate(pairs):
                if n + 1 < len(pairs):
                    emit_S(n + 1)
                ob = 6 + cp
                pr3 = n % 3
                for u in range(2):
                    j = 2 * jp + u
                    P.op("pe", lambda e, j=j, u=u: e.matmul(PS[ob][:, :], lhsT=Vp[:, j, :], rhs=PT[:, pr3, u * 512:(u + 1) * 512], start=(j == 0), stop=(j == 15)), [Vpb[j], Vp1, PTb[pr3]], [PSb[ob]])
                if jp == 7:
                    P.op("dve", lambda e: e.reciprocal(out=rc[96:128, 0, :], in_=PS[ob][96:128, :]), [PSb[ob]], [rcb[cp]])
                    for g3 in range(3):
                        P.op("dve", lambda e, g3=g3: e.tensor_tensor(out=vv[32 * g3:32 * g3 + 32, cp, :], in0=PS[ob][32 * g3:32 * g3 + 32, :], in1=rc[96:128, 0, :], op=ALU.mult),
                             [PSb[ob], rcb[cp]], [vvb[cp]])
                    if cp == 1:
                        P.op("dve", lambda e: e.scalar_tensor_tensor(out=vv[0:96, 0, :], in0=vv[0:96, 1, :], scalar=der[0:96, 4 + l:5 + l], in1=vv[0:96, 0, :], op0=ALU.mult, op1=ALU.add),
                             [vvb[0], vvb[1], der_b], [vvb[0]])
                        P.op("act", lambda e: e.activation(out=sqv[0:96, :], in_=vv[0:96, 0, :], func=AF.Square), [vvb[0]], [sqvb])
                        P.op("pe", lambda e: e.matmul(PS[7][0:96, :], lhsT=ones[0:96, 0:96], rhs=sqv[0:96, :], start=True, stop=True), [sqvb, cB_b], [PSb[7]])
                        P.op("act", lambda e: e.activation(out=tv[0:96, :], in_=PS[7][0:96, :], func=AF.Ln, scale=1.0 / 96, bias=cb_eps(1)[0:96]), [PSb[7], pF_b], [tvb])
                        P.op("act", lambda e: e.activation(out=tv[0:96, :], in_=tv[0:96, :], func=AF.Exp, scale=-0.5), [tvb], [tvb])
                        P.op("dve", lambda e: e.scalar_tensor_tensor(out=mix(3 + h)[0:96, qb * 512:(qb + 1) * 512], in0=vv[0:96, 0, :], scalar=gsub[0:96, l, h:h + 1], in1=tv[0:96, :],
                                                                     op0=ALU.mult, op1=ALU.mult), [vvb[0], tvb, der_b], [mixb[3 + h][qb][0]])

        def job_C(l, pr_, hT, hTb, mix, mixb, scope):
            from contextlib import ExitStack as _ES
            QK = sb("c_qk", [128, 4, S], BF16, scope); QKb = mkbufs("c_qk", 4, 4)
            GS = sb("c_gs", [128, S], BF16, scope); GSb = mkbufs("c_gs", 4)
            KV = sb("c_kv", [128, 2, 4096], BF16, scope); KVb = mkbufs("c_kv", 2)
            Dn = sb("c_dn", [128, 2, 64], F32, scope); Dnb = mkbufs("c_dn", 2)
            Vt = sb("c_vt", [128, 16, 128], BF16, scope); Vtb = mkbufs("c_vt", 16)
            KVv = KV[:, :, :].rearrange("p d (v n) -> p d v n", n=64)
            lb_ap = der[:, 8 + pr_ * DEPTH + l:8 + pr_ * DEPTH + l + 1]
            om_ap = der[:, 16 + pr_ * DEPTH + l:16 + pr_ * DEPTH + l + 1]
            cm = cb("cmask")
            bm = cb("blkm")
            sc = scope.enter_context(_ES())
            W = sb("c_w", [128, 8, 640], BF16, sc); Wb = mkbufs("c_w", 3)
            load_w(wslots["wC"][0], W, Wb, wC_d[l, pr_], 8)
            Vk = sb("c_vk", [128, 2, 512], BF16, sc); Vkb = mkbufs("c_vk", 2)
            NT = 9
            tm = sb("c_tm", [128, NT, 512], F32, sc); tmb = mkbufs("c_tm", NT)
            khT = sb("c_kht", [128, 2, 512], BF16, sc); khTb = mkbufs("c_kht", 2)
            kh = sb("c_kh", [128, 2, 128], BF16, sc); khb = mkbufs("c_kh", 2)
            khn = 0
            def emit_proj(tb):
                cs = slice(tb * 512, (tb + 1) * 512)
                for grp in range(4):
                    for c in range(8):
                        P.op("pe", lambda e, c=c, grp=grp: e.matmul(PS[grp][:, :], lhsT=W[:, c, grp * 128:(grp + 1) * 128], rhs=hT[:, c, cs], start=(c == 0), stop=(c == 7)),
                             [hTb[c][tb]] + Wb, [PSb[grp]])
                for t4 in range(4):
                    i = tb * 4 + t4
                    pb = 4 + (i % 2)
                    ts = slice(i * 128, (i + 1) * 128)
                    for c in range(8):
                        P.op("pe", lambda e, c=c: e.matmul(PS[pb][:, 0:128], lhsT=hT[:, c, ts], rhs=W[:, c, 512:640], start=(c == 0), stop=(c == 7)), [hTb[c][tb]] + Wb, [PSb[pb]])
                    P.op("act", lambda e: e.activation(out=Vt[:, i, :], in_=PS[pb][:, 0:128], func=AF.Copy), [PSb[pb]], [Vtb[i]])

            def emit_head(tb):
                cs = slice(tb * 512, (tb + 1) * 512)
                qs, sf, sbw = 0, 1, 2
                P.op("act", lambda e: e.activation(out=tm[:, qs, :], in_=PS[0][:, :], func=AF.Silu), [PSb[0]], [tmb[qs]])
                P.op("act", lambda e: e.activation(out=GS[:, cs], in_=PS[3][:, :], func=AF.Silu), [PSb[3]], [GSb[tb]])
                P.op("act", lambda e: e.activation(out=tm[:, sf, :], in_=PS[1][:, :], func=AF.Sigmoid), [PSb[1]], [tmb[sf]])
                P.op("act", lambda e: e.activation(out=tm[:, sbw, :], in_=PS[2][:, :], func=AF.Sigmoid), [PSb[2]], [tmb[sbw]])

            def emit_rest(tb):
                nonlocal khn
                cs = slice(tb * 512, (tb + 1) * 512)
                qs, sf, sbw = 0, 1, 2
                def chain(d_):
                    f_ = sf if d_ == 0 else sbw
                    k_, lf_, b_ = (3, 4, 5) if d_ == 0 else (6, 7, 8)
                    t_, e_ = b_, f_
                    P.op("pool", lambda e: e.tensor_scalar(out=tm[:, f_, :], in0=tm[:, f_, :], scalar1=om_ap, scalar2=lb_ap, op0=ALU.mult, op1=ALU.add), [tmb[f_], der_b], [tmb[f_]]); yield
                    P.op("dve", lambda e: e.tensor_scalar(out=tm[:, f_, :], in0=tm[:, f_, :], scalar1=1e-20, scalar2=None, op0=ALU.max), [tmb[f_]], [tmb[f_]]); yield
                    P.op("pool", lambda e: e.tensor_scalar(out=tm[:, k_, :], in0=tm[:, f_, :], scalar1=-1.0, scalar2=1.0, op0=ALU.mult, op1=ALU.add), [tmb[f_]], [tmb[k_]])
                    P.op("act", lambda e: e.activation(out=tm[:, lf_, :], in_=tm[:, f_, :], func=AF.Ln), [tmb[f_]], [tmb[lf_]]); yield
                    if d_ == 0:
                        P.op("dve", lambda e: e.tensor_tensor_scan(out=tm[:, b_, :], data0=cm[:, 0, :], data1=tm[:, lf_, :], initial=0.0, op0=ALU.mult, op1=ALU.add), [tmb[lf_], cB_b], [tmb[b_]])
                    else:
                        P.op("dve", lambda e: e.tensor_tensor_scan(out=tm[:, b_, ::-1], data0=cm[:, 1, ::-1], data1=tm[:, lf_, ::-1], initial=0.0, op0=ALU.mult, op1=ALU.add),
                             [tmb[lf_], cB_b], [tmb[b_]])
                    yield
                    bv = tm[:, b_, :].rearrange("p (n t) -> p n t", t=32)
                    be = bv[:, :, 31:32] if d_ == 0 else bv[:, :, 0:1]
                    P.op("pool", lambda e: e.tensor_tensor(out=tm[:, lf_, :].rearrange("p (n t) -> p n t", t=32), in0=be.to_broadcast([128, 16, 32]), in1=bv, op=ALU.subtract), [tmb[b_], tmb[lf_]], [tmb[lf_]])
                    P.op("act", lambda e: e.activation(out=Dn[:, d_, tb * 16:(tb + 1) * 16], in_=be.rearrange("p n o -> p (n o)"), func=AF.Exp), [tmb[b_]], [Dnb[d_]]); yield
                    P.op("act", lambda e: e.activation(out=tm[:, lf_, :], in_=tm[:, lf_, :], func=AF.Exp), [tmb[lf_]], [tmb[lf_]])
                    P.op("dve", lambda e: e.tensor_scalar(out=tm[:, b_, :], in0=tm[:, b_, :], scalar1=-60.0, scalar2=None, op0=ALU.max), [tmb[b_]], [tmb[b_]]); yield
                    P.op("pool", lambda e: e.tensor_tensor(out=khT[:, d_, :], in0=tm[:, k_, :], in1=tm[:, lf_, :], op=ALU.mult), [tmb[k_], tmb[lf_]], [khTb[d_]])
                    P.op("act", lambda e: e.activation(out=tm[:, e_, :], in_=tm[:, b_, :], func=AF.Exp, scale=-1.0), [tmb[b_]], [tmb[e_]])
                    P.op("act", lambda e: e.activation(out=tm[:, t_, :], in_=tm[:, b_, :], func=AF.Exp), [tmb[b_]], [tmb[t_]]); yield
                    P.op("dve", lambda e: e.tensor_tensor(out=QK[:, 2 * d_, cs], in0=tm[:, qs, :], in1=tm[:, t_, :], op=ALU.mult), [tmb[qs], tmb[t_]], [QKb[2 * d_][tb]])
                    P.op("dve", lambda e: e.tensor_tensor(out=QK[:, 2 * d_ + 1, cs], in0=tm[:, k_, :], in1=tm[:, e_, :], op=ALU.mult), [tmb[k_], tmb[e_]], [QKb[2 * d_ + 1][tb]]); yield

                gens = [chain(0), chain(1)]
                while gens:
                    for g_ in list(gens):
                        try:
                            next(g_)
                        except StopIteration:
                            gens.remove(g_)
                for t4 in range(4):
                    i = tb * 4 + t4
                    vkk = i % 2
                    P.op("dve", lambda e: e.tensor_tensor(out=Vk[:, vkk, :].rearrange("p (n v) -> p n v", n=4), in0=bcast(Vt[:, i, :], 1, [128, 4, 128]), in1=bm, op=ALU.mult),
                         [Vtb[i], cB_b], [Vkb[vkk]])
                    for d_ in range(2):
                        pb = 6 + (khn % 2)
                        kr = khn % 2
                        khn += 1
                        tp = PS[pb][:, :].bitcast(BF16)
                        P.op("pe", lambda e: e.transpose(tp[:, 0:128], khT[:, d_, t4 * 128:(t4 + 1) * 128], ident), [khTb[d_], cB_b], [PSb[pb]])
                        P.op("act", lambda e: e.activation(out=kh[:, kr, :], in_=tp[:, 0:128], func=AF.Copy), [PSb[pb]], [khb[kr]])
                        P.op("pe", lambda e: e.matmul(PS[pb][:, :], lhsT=kh[:, kr, :], rhs=Vk[:, vkk, :], start=True, stop=True), [khb[kr], Vkb[vkk]], [PSb[pb]])
                        for hh in range(2):
                            hs = slice(64 * hh, 64 * hh + 64)
                            src = PS[pb][hs, :].rearrange("p (n v) -> p v n", n=4)[:, 64 * hh:64 * hh + 64, :]
                            P.op("dve", lambda e, hs=hs, src=src: e.tensor_copy(out=KVv[hs, d_, :, 4 * i:4 * i + 4], in_=src), [PSb[pb]], [KVb[d_]])

            emit_proj(0)
            for tb in range(4):
                emit_head(tb)
                if tb + 1 < 4:
                    emit_proj(tb + 1)
                emit_rest(tb)
            P.barrier()
            sc.close()
            sc = scope.enter_context(_ES())
            Df = sb("c_df", [128, 4096], BF16, sc); Dfb = Buf("c_df")
            Dfv = Df[:, :].rearrange("p (v n) -> p v n", n=64)
            for d_ in range(2):
                P.op("act", lambda e: e.activation(out=Dfv, in_=bcast(Dn[:, d_, :], 1, [128, 64, 64]), func=AF.Copy), [Dnb[d_]], [Dfb])
                zc = 0 if d_ == 0 else 63
                P.op("pool", lambda e: e.memset(Dfv[:, :, zc:zc + 1], 0.0), [], [Dfb])
                if d_ == 0:
                    P.op("dve", lambda e: e.tensor_tensor_scan(out=KV[:, 0, :], data0=Df[:, :], data1=KV[:, 0, :], initial=0.0, op0=ALU.mult, op1=ALU.add), [Dfb, KVb[0]], [KVb[0]])
                else:
                    P.op("dve", lambda e: e.tensor_tensor_scan(out=KV[:, 1, ::-1], data0=Df[:, ::-1], data1=KV[:, 1, ::-1], initial=0.0, op0=ALU.mult, op1=ALU.add), [Dfb, KVb[1]], [KVb[1]])
            P.barrier()
            sc.close()
            Am = sb("c_am", [128, 2, 2, 256], BF16, scope); Amb = mkbufs("c_am", 2, 2)
            oc = sb("c_oc", [128, 2, 512], F32, scope); ocb = mkbufs("c_oc", 2)
            sqo = sb("c_sqo", [128, 512], BF16, scope); sqob = Buf("c_sqo")
            to = sb("c_to", [128, 2, 512], F32, scope); tob = mkbufs("c_to", 2)
            mk = cb("hmask")
            gco = pf("gco")
            def emit_A(i):
                ts = slice(i * 128, (i + 1) * 128)
                r = i % 2
                tb = i // 4
                for hh in range(2):
                    hs = slice(64 * hh, 64 * hh + 64)
                    ab = 2 * r + hh
                    for d_ in range(2):
                        P.op("pe", lambda e, d_=d_, hs=hs, ab=ab: e.matmul(PS[ab][:, d_ * 128:(d_ + 1) * 128], lhsT=QK[hs, 2 * d_ + 1, ts], rhs=QK[hs, 2 * d_, ts], start=True, stop=True),
                             [QKb[2 * d_ + 1][tb], QKb[2 * d_][tb]], [PSb[ab]])
                    P.op("dve", lambda e, hh=hh, ab=ab: e.tensor_tensor(out=Am[:, r, hh, :], in0=PS[ab][:, 0:256], in1=mk.rearrange("p d t -> p (d t)"), op=ALU.mult), [PSb[ab], cB_b], [Amb[r][hh]])

            def emit_O(i):
                ts = slice(i * 128, (i + 1) * 128)
                r = i % 2
                tb = i // 4
                for hh in range(2):
                    ob = 4 + r + 2 * hh
                    hs = slice(64 * hh, 64 * hh + 64)
                    for d_ in range(2):
                        P.op("pe", lambda e, d_=d_, hs=hs, hh=hh: e.matmul(PS[ob][hs, 0:128], lhsT=Vt[:, i, hs], rhs=Am[:, r, hh, d_ * 128:(d_ + 1) * 128], start=(d_ == 0), stop=False),
                             [Vtb[i], Amb[r][hh]], [PSb[ob]])
                    inter = [(d_, n4, (4 * i + n4 - 1) if d_ == 0 else (4 * i + n4 + 1)) for d_ in range(2) for n4 in range(4)]
                    inter = [(d_, n4, idx) for (d_, n4, idx) in inter if 0 <= idx <= 63]
                    for k_, (d_, n4, idx) in enumerate(inter):
                        P.op("pe", lambda e, d_=d_, hs=hs, n4=n4, idx=idx, k_=k_: e.matmul(PS[ob][hs, 32 * n4:32 * n4 + 32], lhsT=KVv[hs, d_, :, idx], rhs=QK[hs, 2 * d_, i * 128 + 32 * n4:i * 128 + 32 * n4 + 32],
                                                                                           start=False, stop=(k_ == len(inter) - 1)), [KVb[d_], QKb[2 * d_][tb]], [PSb[ob]])
                t4 = i % 4
                rr = tb % 2
                for hh in range(2):
                    hs = slice(64 * hh, 64 * hh + 64)
                    P.op("act", lambda e, hs=hs, hh=hh: e.activation(out=oc[hs, rr, t4 * 128:(t4 + 1) * 128], in_=PS[4 + r + 2 * hh][hs, 0:128], func=AF.Copy), [PSb[4 + r + 2 * hh]], [ocb[rr]])
                if t4 == 3:
                    cs = slice(tb * 512, (tb + 1) * 512)
                    P.op("act", lambda e: e.activation(out=sqo[:, :], in_=oc[:, rr, :], func=AF.Square), [ocb[rr]], [sqob])
                    P.op("pe", lambda e: e.matmul(PS[0][:, :], lhsT=blk1, rhs=sqo[:, :], start=True, stop=True), [sqob, cB_b], [PSb[0]])
                    P.op("act", lambda e: e.activation(out=to[:, rr, :], in_=PS[0][:, :], func=AF.Ln, scale=1.0 / 64, bias=cb_eps(1)), [PSb[0], pF_b], [tob[rr]])
                    P.op("act", lambda e: e.activation(out=to[:, rr, :], in_=to[:, rr, :], func=AF.Exp, scale=-0.5), [tob[rr]], [tob[rr]])
                    P.op("dve", lambda e: e.scalar_tensor_tensor(out=to[:, rr, :], in0=oc[:, rr, :], scalar=gco[:, l, pr_:pr_ + 1], in1=to[:, rr, :], op0=ALU.mult, op1=ALU.mult),
                         [ocb[rr], tob[rr], pF_b], [tob[rr]])
                    P.op("dve", lambda e: e.tensor_tensor(out=mix(7 + pr_)[:, cs], in0=to[:, rr, :], in1=GS[:, cs], op=ALU.mult), [tob[rr], GSb[tb]], [mixb[7 + pr_][tb][0]])


            for i in range(16):
                emit_A(i)
                emit_O(i)

        def out_proj(l, mix, mixb, chunks, scope):
            WO = sb("o_w", [128, 9, 1024], BF16, scope); WOb = mkbufs("o_w", 9)
            P.dma_group("pool", wslots["wO"][0], [(WO[:, ch, :], wO_d[l, ch]) for ch in chunks], writes=[WOb[ch] for ch in chunks])
            for tb in range(4):
                cs = slice(tb * 512, (tb + 1) * 512)
                for oc_ in range(8):
                    pb = oc_ % 4
                    for n, ch in enumerate(chunks):
                        K = 96 if 3 <= ch <= 6 else 128
                        P.op("pe", lambda e, ch=ch, K=K, n=n: e.matmul(PS[pb][:, :], lhsT=WO[0:K, ch, oc_ * 128:(oc_ + 1) * 128], rhs=mix(ch)[0:K, cs], start=(n == 0), stop=(n == len(chunks) - 1)),
                             [WOb[ch]] + mixb[ch][tb], [PSb[pb]])
                    P.op("dve", lambda e: e.tensor_tensor(out=X[:, oc_, cs], in0=PS[pb][:, :], in1=X[:, oc_, cs], op=ALU.add), [PSb[pb], Xb[oc_][tb]], [Xb[oc_][tb]])

        def attention_phase(l):
            with ExitStack() as sc1:
                hT = sb("hT", [128, 8, S], BF16, sc1); hTb = mkbufs("hT", 8, 4)
                mixC = sb("mixC", [128, 2, S], BF16, sc1)
                mixb = [[[Buf(f"mix{c}_{t}_{u}") for u in range(2)] for t in range(4)] for c in range(9)]
                holder = {}

                def mix(ch):
                    if ch >= 7:
                        return mixC[:, ch - 7, :]
                    return holder["ab"][:, ch, :]

                with ExitStack() as sc2:
                    rmsnorm_T("gattn", l, hT, hTb, sc2)
                    P.barrier()
                chunks = []
                if "C" in stages:
                    for pr_ in range(2):
                        with ExitStack() as sc2:
                            job_C(l, pr_, hT, hTb, mix, mixb, sc2)
                            P.barrier()
                    chunks += [7, 8]
                holder["ab"] = sb("mixAB", [128, 7, S], BF16, sc1)
                if "A" in stages:
                    for g in range(2):
                        with ExitStack() as sc2:
                            job_A(l, g, hT, hTb, mix, mixb, sc2)
                            P.barrier()
                    chunks += [0, 1, 2]
                if "B" in stages:
                    for h in range(4):
                        with ExitStack() as sc2:
                            job_B(l, h, hT, hTb, mix, mixb, sc2)
                            P.barrier()
                    chunks += [3, 4, 5, 6]
                if chunks:
                    with ExitStack() as sc2:
                        out_proj(l, mix, mixb, sorted(chunks), sc2)
                        P.barrier()

        def ffn_phase(l):
            with ExitStack() as sc1:
                hT = sb("h2T", [128, 8, S], BF16, sc1); hTb = mkbufs("h2T", 8, 4)
                with ExitStack() as sc2:
                    rmsnorm_T("gffn", l, hT, hTb, sc2)
                    P.barrier()
                U = sb("f_u", [128, NJ, 1024], BF16, sc1); Ub = mkbufs("f_u", NJ, 2)
                WU = sb("f_wu", [128, 2, 8, 256], BF16, sc1); WUb = mkbufs("f_wu", 2, 1)
                WD = sb("f_wd", [128, 2, NJ, 128], BF16, sc1); WDb = mkbufs("f_wd", 2, 2)
                gp = sb("f_gp", [128, 2, 1026], F32, sc1); gpb = mkbufs("f_gp", 2)
                cv = sb("f_cv", [128, 2, 1024], F32, sc1); cvb = mkbufs("f_cv", 2)
                P.op("pool", lambda e: e.memset(gp[:, :, :], 0.0), [], gpb)
                cw = pf("convw")
                cbv = pf("convb")
                nload = 0
                for half in range(2):
                    h0 = half * 1024
                    g0 = 0 if half == 0 else 1023
                    c0 = 1 if half == 0 else 0
                    zc = 0 if half == 0 else 1025
                    for gi in range(2):
                        P.op("pool", lambda e, gi=gi: e.memset(gp[:, gi, zc:zc + 1], 0.0), [], [gpb[gi]])
                    for j in range(NJ):
                        ws = nload % 2
                        nload += 1
                        load_w(wslots["wU"][ws], WU[:, ws], WUb[ws], wU_d[l, j], 8)
                        gr = j % 2
                        blocks = [(0, 342), (342, 342), (684, 341)]
                        for bi, (o, nn) in enumerate(blocks):
                            for c in range(8):
                                P.op("pe", lambda e, c=c, o=o, nn=nn, bi=bi: e.matmul(PS[bi][:, 0:nn], lhsT=WU[:, ws, c, 0:128], rhs=hT[:, c, g0 + o:g0 + o + nn], start=(c == 0), stop=(c == 7)),
                                     WUb[ws] + [hTb[c][t] for t in range(4)], [PSb[bi]])
                            P.op("act", lambda e, o=o, nn=nn, bi=bi: e.activation(out=gp[:, gr, c0 + o:c0 + o + nn], in_=PS[bi][:, 0:nn], func=AF.Copy), [PSb[bi]], [gpb[gr]])
                        for vb in range(2):
                            pb = 3 + vb + 2 * (j % 2)
                            for c in range(8):
                                P.op("pe", lambda e, c=c, vb=vb, pb=pb: e.matmul(PS[pb][:, :], lhsT=WU[:, ws, c, 128:256], rhs=hT[:, c, h0 + vb * 512:h0 + (vb + 1) * 512], start=(c == 0), stop=(c == 7)),
                                     WUb[ws] + [hTb[c][half * 2 + vb]], [PSb[pb]])
                        P.op("act", lambda e: e.activation(out=cv[:, gr, :], in_=gp[:, gr, 1:1025], func=AF.Identity, scale=cw[:, l, j, 1:2], bias=cbv[:, l, j:j + 1]), [gpb[gr], pF_b], [cvb[gr]])
                        P.op("dve", lambda e: e.scalar_tensor_tensor(out=cv[:, gr, :], in0=gp[:, gr, 0:1024], scalar=cw[:, l, j, 0:1], in1=cv[:, gr, :], op0=ALU.mult, op1=ALU.add), [gpb[gr], cvb[gr], pF_b], [cvb[gr]])
                        P.op("dve", lambda e: e.scalar_tensor_tensor(out=cv[:, gr, :], in0=gp[:, gr, 2:1026], scalar=cw[:, l, j, 2:3], in1=cv[:, gr, :], op0=ALU.mult, op1=ALU.add), [gpb[gr], cvb[gr], pF_b], [cvb[gr]])
                        P.op("act", lambda e: e.activation(out=cv[:, gr, :], in_=cv[:, gr, :], func=AF.Gelu), [cvb[gr]], [cvb[gr]])
                        for vb in range(2):
                            pb = 3 + vb + 2 * (j % 2)
                            P.op("dve", lambda e, vb=vb, pb=pb: e.tensor_tensor(out=U[:, j, vb * 512:(vb + 1) * 512], in0=PS[pb][:, :], in1=cv[:, gr, vb * 512:(vb + 1) * 512], op=ALU.mult),
                                 [PSb[pb], cvb[gr]], [Ub[j][vb]])
                    for oc_ in range(8):
                        ws = oc_ % 2
                        P.dma_group("pool", wslots["wD"][ws], [(WD[:, ws, j0:j0 + 11, :], wD_d[l, oc_, :, j0:j0 + 11, :]) for j0 in range(0, NJ, 11)], writes=WDb[ws])
                        for vb in range(2):
                            pb = (oc_ * 2 + vb) % 4
                            tb = half * 2 + vb
                            for j in range(NJ):
                                P.op("pe", lambda e, j=j, vb=vb, pb=pb: e.matmul(PS[pb][:, :], lhsT=WD[:, ws, j, :], rhs=U[:, j, vb * 512:(vb + 1) * 512], start=(j == 0), stop=(j == NJ - 1)),
                                     WDb[ws] + [Ub[j][vb]], [PSb[pb]])
                            P.op("dve", lambda e, vb=vb, pb=pb, tb=tb: e.tensor_tensor(out=X[:, oc_, tb * 512:(tb + 1) * 512], in0=PS[pb][:, :], in1=X[:, oc_, tb * 512:(tb + 1) * 512], op=ALU.add),
                                 [PSb[pb], Xb[oc_][tb]], [Xb[oc_][tb]])
                P.barrier()

        allX = [Xb[c][t] for c in range(8) for t in range(4)]
        for s in range(nseq):
            P.dma_group("sp", s_x, [(X[:, c, :], xT_d[s, c]) for c in range(8)], writes=allX)
            for l in range(depth):
                if any(k in stages for k in "ABC"):
                    attention_phase(l)
                if "F" in stages:
                    ffn_phase(l)
            P.dma_group("sp", s_o, [(out_d[s, c], X[:, c, :]) for c in range(8)], reads=allX)
        P.barrier()
        print(f"[build] ops={P.nop} waits={P.nwait} cnt={P.cnt}")
    return nc


def _layout(items):
    off = {}
    o = 0
    for name, shp in items:
        off[name] = (o, tuple(shp))
        o += int(np.prod(shp))
    return off, o


PF_ITEMS = [
    ("gattn", (DEPTH, 8)), ("gffn", (DEPTH, 8)), ("gA", (DEPTH, 256)), ("gB", (DEPTH, 192)), ("bsub", (DEPTH, 4)),
    ("blam", (DEPTH, 4, 48)), ("clb", (2, DEPTH)), ("gco", (DEPTH, 2)), ("convw", (DEPTH, NJ, 3)), ("convb", (DEPTH, NJ)),
    ("cos", (16, 32)), ("sin", (16, 32)), ("sel", (96,)), ("epsv", (4,)),
]
PF_OFF, PF_COLS = _layout(PF_ITEMS)
CB_ITEMS = [("ident", (128,)), ("ones", (128,)), ("blk1", (128,)), ("dg", (4, 128)), ("cmask", (2, 512)), ("blkm", (4, 128)), ("hmask", (2, 128)), ("aug", (4, 16, 32))]
CB_OFF, CB_COLS = _layout(CB_ITEMS)


def _const_tables():
    p = np.arange(128)
    cbt = np.zeros((128, CB_COLS), np.float32)

    def put(name, arr):
        o, shp = CB_OFF[name]
        cbt[:, o:o + int(np.prod(shp))] = np.asarray(arr, np.float32).reshape(128, -1)

    put("ident", np.eye(128))
    put("ones", np.ones((128, 128)))
    put("blk1", (p[:, None] // 64 == p[None, :] // 64))
    put("dg", np.stack([-2.0 * SLOPES[h] * np.maximum(p[:, None] - p[None, :], 0) for h in range(4)], axis=1))
    t = np.arange(512)
    cm = np.stack([(t % 32 != 0), (t % 32 != 31)]).astype(np.float32)
    put("cmask", np.broadcast_to(cm[None], (128, 2, 512)))
    put("blkm", np.stack([np.broadcast_to((p // 32 == n)[:, None], (128, 128)) for n in range(4)], axis=1))
    same = (p[:, None] // 32 == p[None, :] // 32)
    put("hmask", np.stack([same & (p[:, None] <= p[None, :]), same & (p[:, None] >= p[None, :])], axis=1))
    aug = np.zeros((128, 4, 16, 4, 2, 4), np.float32)
    for h in range(4):
        s = SLOPES[h]
        for ti in range(16):
            for u, sg in ((0, 1.0), (1, -1.0)):
                for gq in (0, 1):
                    aug[:, h, ti, gq, u, 0] = -sg * s * 128 * ti
                    aug[:, h, ti, gq, u, 1] = -sg * s * p
                    aug[:, h, ti, gq, u, 2] = 1.0
                    aug[:, h, ti, gq, u, 3] = 1.0
                for gk in (2, 3):
                    aug[:, h, ti, gk, u, 0] = 1.0
                    aug[:, h, ti, gk, u, 1] = 1.0
                    aug[:, h, ti, gk, u, 2] = sg * s * 128 * ti
                    aug[:, h, ti, gk, u, 3] = sg * s * p
    put("aug", aug)
    return cbt


def _pack_pf(inp):
    pft = np.zeros((128, PF_COLS), np.float32)
    p = np.arange(128)

    def put(name, arr):
        o, shp = PF_OFF[name]
        pft[:, o:o + int(np.prod(shp))] = np.asarray(arr, np.float32).reshape(128, -1)

    put("gattn", inp["attn_norm"].reshape(DEPTH, 8, 128).transpose(2, 0, 1))
    put("gffn", inp["ffn_norm"].reshape(DEPTH, 8, 128).transpose(2, 0, 1))
    gA = np.concatenate([inp["a_q_norm"]] * 3 + [inp["a_k_norm"]], axis=1)
    put("gA", np.broadcast_to(gA[None], (128, DEPTH, 256)))
    gB = np.concatenate([inp["b_q_norm"]] * 2 + [inp["b_k_norm"]] * 2, axis=1)
    put("gB", np.broadcast_to(gB[None], (128, DEPTH, 192)))
    bs = np.zeros((128, DEPTH, 4), np.float32)
    bs[0:96] = inp["b_sub_norm"].reshape(DEPTH, 4, 96).transpose(2, 0, 1)
    put("bsub", bs)
    put("blam", np.broadcast_to(inp["b_lambda"][None], (128, DEPTH, 4, 48)))
    put("clb", inp["c_lb_logits"].reshape(DEPTH, 2, 128).transpose(2, 1, 0))
    put("gco", inp["c_out_norm"].reshape(DEPTH, 2, 128).transpose(2, 0, 1))
    put("convw", inp["conv_w"].reshape(DEPTH, 3, NJ, 128).transpose(3, 0, 2, 1))
    put("convb", inp["conv_b"].reshape(DEPTH, NJ, 128).transpose(2, 0, 1))
    tok = (np.arange(16)[None, :] * 128 + p[:, None])
    row, col = tok // 64, tok % 64
    inv = (10000.0 ** (-np.arange(0, 32, 2, dtype=np.float32) / 32)).astype(np.float32)
    ang = np.stack([row, col], axis=-1).astype(np.float32)[..., None] * inv
    put("cos", np.cos(ang).astype(np.float32))
    put("sin", np.sin(ang).astype(np.float32))
    sel = np.zeros((128, 96), np.float32)
    sel[96, :] = 1.0
    put("sel", sel)
    ev = np.zeros((128, 4), np.float32)
    ev[:, 0], ev[:, 1], ev[:, 2] = EPS, 64 * EPS, 48 * EPS
    put("epsv", ev)
    return pft


def _pack_weights(inp):
    w_in = inp["w_in"]

    def tile_cols(cols):
        w = w_in[:, :, cols]
        return np.ascontiguousarray(w.reshape(DEPTH, 8, 128, len(cols)).transpose(0, 2, 1, 3))

    wA = np.stack([tile_cols(np.r_[192 * g:192 * g + 192, 384 + 64 * g:384 + 64 * g + 64, 512 + 64 * g:512 + 64 * g + 64]) for g in range(2)], axis=1)
    wB = np.stack([tile_cols(np.r_[640 + 96 * h:640 + 96 * h + 96, 1024 + 96 * h:1024 + 96 * h + 96, 1408 + 96 * h:1408 + 96 * h + 96]) for h in range(4)], axis=1)
    wC = np.stack([tile_cols(np.r_[1792 + 128 * q:1792 + 128 * q + 128, 2048 + 128 * q:2048 + 128 * q + 128, 2304 + 128 * q:2304 + 128 * q + 128,
                                   2816 + 128 * q:2816 + 128 * q + 128, 2560 + 128 * q:2560 + 128 * q + 128]) for q in range(2)], axis=1)
    w_out = inp["w_out"]
    wO = np.zeros((DEPTH, 9, 128, 1024), np.float32)
    for c in range(3):
        wO[:, c] = w_out[:, 128 * c:128 * c + 128]
    for h in range(4):
        wO[:, 3 + h, 0:96] = w_out[:, 384 + 96 * h:384 + 96 * h + 96]
    for q in range(2):
        wO[:, 7 + q] = w_out[:, 768 + 128 * q:768 + 128 * q + 128]
    w_up = inp["w_up"]
    wg = w_up[:, :, :DFF].reshape(DEPTH, 8, 128, NJ, 128)
    wv = w_up[:, :, DFF:].reshape(DEPTH, 8, 128, NJ, 128)
    wU = np.ascontiguousarray(np.concatenate([wg, wv], axis=-1).transpose(0, 3, 2, 1, 4))
    wD = np.ascontiguousarray(inp["w_down"].reshape(DEPTH, NJ, 128, 8, 128).transpose(0, 3, 2, 1, 4))
    return dict(wA=wA, wB=wB, wC=wC, wO=wO, wU=wU, wD=wD)


_CACHE = {}


def kernel(**inputs):
    inp = {k: np.asarray(v, np.float32) for k, v in inputs.items()}
    x = inp["x"]
    shared = _pack_weights(inp)
    shared["pF"] = _pack_pf(inp)
    shared["cB"] = _const_tables()
    if "nc" not in _CACHE:
        _CACHE["nc"] = build_program()
    nc = _CACHE["nc"]
    in_maps = []
    for c in range(NCORES):
        xs = x[c * NSEQ:(c + 1) * NSEQ]
        xT = np.ascontiguousarray(xs.transpose(0, 2, 1)).reshape(NSEQ, 8, 128, S)
        m = dict(shared)
        m["xT"] = xT
        in_maps.append(m)
    res = run_bass_kernel_spmd(nc, in_maps, core_ids=list(range(NCORES)))
    outs = []
    for c in range(NCORES):
        o = np.asarray(res.results[c]["outT"]).reshape(NSEQ, D, S)
        outs.append(o.transpose(0, 2, 1))
    return np.ascontiguousarray(np.concatenate(outs, axis=0)).astype(np.float32)
```

```python
import math
import numpy as np
import ml_dtypes
import concourse.bass as bass
import concourse.mybir as mybir
from concourse.bass_utils import run_bass_kernel_spmd

F32 = mybir.dt.float32
BF16 = mybir.dt.bfloat16
ALU = mybir.AluOpType
AF = mybir.ActivationFunctionType
AX = mybir.AxisListType

D = 1024
S = 2048
DEPTH = 4
NCORES = 8
NSEQ = 4
EPS = 1e-6
DFF = 2816
NJ = DFF // 128
SLOPES = [2.0 ** (-8.0 * (h + 1) / 4) for h in range(4)]
LAM_INIT = [0.8 - 0.6 * math.exp(-0.3 * l) for l in range(DEPTH)]
ENG = ("pe", "act", "dve", "pool", "sp")


class Buf:
    __slots__ = ("name", "w", "rd")

    def __init__(self, name):
        self.name = name
        self.w = None
        self.rd = {}


def mkbufs(name, *dims):
    if not dims:
        return Buf(name)
    return [mkbufs(f"{name}_{i}", *dims[1:]) for i in range(dims[0])]


class Slot:
    def __init__(self, sem, key):
        self.sem = sem
        self.key = key
        self.cnt = 0


class Prog:
    def __init__(self, nc, es):
        self.nc = nc
        self.e = {"pe": nc.tensor, "act": nc.scalar, "dve": nc.vector, "pool": nc.gpsimd, "sp": nc.sync}
        self.sem = {k: es.enter_context(nc.semaphore("sem_" + k)) for k in ENG}
        self.cnt = {k: 0 for k in ENG}
        self.seen = {k: {} for k in ENG}
        self.es = es
        self.slots = []
        self.nwait = 0
        self.nop = 0

    def slot(self, name):
        s = Slot(self.es.enter_context(self.nc.semaphore("dsem_" + name)), "slot_" + name + str(len(self.slots)))
        self.slots.append(s)
        return s

    def _wait(self, e, tok):
        key, val = tok
        if self.seen[e].get(key, 0) >= val:
            return
        sem = self.sem[key] if key in self.sem else self._slot_by_key[key].sem
        self.e[e].wait_ge(sem, val)
        self.seen[e][key] = val
        self.nwait += 1

    @property
    def _slot_by_key(self):
        return {s.key: s for s in self.slots}

    def _deps(self, e, reads, writes):
        toks = {}

        def add(t):
            if t is None:
                return
            k, v = t
            if toks.get(k, 0) < v:
                toks[k] = v

        for b in reads:
            add(b.w)
        for b in writes:
            add(b.w)
            for k, v in b.rd.items():
                add((k, v))
        for k, v in toks.items():
            if e == "pe" and k == "pe":
                continue
            self._wait(e, (k, v))

    def op(self, e, fn, reads=(), writes=(), pe_sync=False):
        self._deps(e, reads, writes)
        if pe_sync and e == "pe" and self.cnt["pe"] > 0:
            self._wait("pe", ("pe", self.cnt["pe"]))
        inst = fn(self.e[e])
        self.cnt[e] += 1
        inst.then_inc(self.sem[e], 1)
        tok = (e, self.cnt[e])
        for b in writes:
            b.w = tok
            b.rd = {}
        for b in reads:
            if b.rd.get(e, 0) < tok[1]:
                b.rd[e] = tok[1]
        self.nop += 1
        return inst

    def dma(self, q, slot, out, in_, reads=(), writes=()):
        self._deps(q, reads, writes)
        inst = self.e[q].dma_start(out=out, in_=in_)
        slot.cnt += 16
        inst.then_inc(slot.sem, 16)
        tok = (slot.key, slot.cnt)
        for b in writes:
            b.w = tok
            b.rd = {}
        for b in reads:
            b.rd[slot.key] = tok[1]
        self.nop += 1

    def dma_group(self, q, slot, pairs, reads=(), writes=()):
        self._deps(q, reads, writes)
        for out, in_ in pairs:
            inst = self.e[q].dma_start(out=out, in_=in_)
            slot.cnt += 16
            inst.then_inc(slot.sem, 16)
            self.nop += 1
        tok = (slot.key, slot.cnt)
        for b in writes:
            b.w = tok
            b.rd = {}
        for b in reads:
            b.rd[slot.key] = tok[1]

    def barrier(self):
        sk = self._slot_by_key
        for e in ENG:
            for f in ENG:
                if f != e and self.cnt[f] > 0:
                    self._wait(e, (f, self.cnt[f]))
            for s in self.slots:
                if s.cnt > 0:
                    self._wait(e, (s.key, s.cnt))


def bcast(ap, axis, shape):
    return ap.unsqueeze(axis).to_broadcast(list(shape))


def build_program(nseq=NSEQ, depth=DEPTH, stages=("C", "A", "B", "F"), debug_out=None):
    from contextlib import ExitStack

    nc = bass.Bass("TRN2", target_bir_lowering=False)
    dt = nc.dram_tensor
    xT_d = dt("xT", [nseq, 8, 128, S], F32, kind="ExternalInput").ap()
    out_d = dt("outT", [nseq, 8, 128, S], F32, kind="ExternalOutput").ap()
    wA_d = dt("wA", [DEPTH, 2, 128, 8, 320], F32, kind="ExternalInput").ap()
    wB_d = dt("wB", [DEPTH, 4, 128, 8, 288], F32, kind="ExternalInput").ap()
    wC_d = dt("wC", [DEPTH, 2, 128, 8, 640], F32, kind="ExternalInput").ap()
    wO_d = dt("wO", [DEPTH, 9, 128, 1024], F32, kind="ExternalInput").ap()
    wU_d = dt("wU", [DEPTH, NJ, 128, 8, 256], F32, kind="ExternalInput").ap()
    wD_d = dt("wD", [DEPTH, 8, 128, NJ, 128], F32, kind="ExternalInput").ap()
    pF_d = dt("pF", [128, PF_COLS], F32, kind="ExternalInput").ap()
    cB_d = dt("cB", [128, CB_COLS], F32, kind="ExternalInput").ap()

    with ExitStack() as es:
        P = Prog(nc, es)

        uid = [0]

        def sb(name, shape, dtype, scope=es):
            uid[0] += 1
            return scope.enter_context(nc.sbuf_tensor(f"{name}_s{uid[0]}", list(shape), dtype))

        X = sb("X", [128, 8, S], F32)
        Xb = mkbufs("X", 8, 4)
        pF = sb("pF", [128, PF_COLS], F32)
        cB = sb("cB", [128, CB_COLS], BF16)
        pF_b, cB_b = Buf("pF"), Buf("cB")
        PSALL = es.enter_context(nc.psum_tensor("psall", [128, 4096], F32))
        PS = [PSALL[:, 512 * i:512 * (i + 1)] for i in range(8)]
        PSb = mkbufs("ps", 8)
        der = sb("der", [128, 64], F32)
        der_b = Buf("der")

        def pf(name, *idx):
            off, shp = PF_OFF[name]
            ap = pF[:, off:off + int(np.prod(shp))]
            if len(shp) > 1:
                names = " ".join(f"a{i}" for i in range(len(shp)))
                ap = ap.rearrange(f"p ({names}) -> p {names}", **{f"a{i}": shp[i] for i in range(len(shp))})
            return ap

        def cb(name):
            off, shp = CB_OFF[name]
            ap = cB[:, off:off + int(np.prod(shp))]
            if len(shp) > 1:
                names = " ".join(f"a{i}" for i in range(len(shp)))
                ap = ap.rearrange(f"p ({names}) -> p {names}", **{f"a{i}": shp[i] for i in range(len(shp))})
            return ap

        s_const = P.slot("const")
        s_const2 = P.slot("const2")
        P.dma("sp", s_const, pF[:], pF_d, writes=[pF_b])
        P.dma_group("pool", s_const2, [(cB[:, i:min(CB_COLS, i + 2048)], cB_d[:, i:min(CB_COLS, i + 2048)]) for i in range(0, CB_COLS, 2048)], writes=[cB_b])

        ident = cb("ident")
        ones = cb("ones")
        blk1 = cb("blk1")

        lamv = pf("blam")
        tmpd = sb("tmpd", [128, 2 * DEPTH * 48 + 64], F32)
        tmpd_b = Buf("tmpd")
        pr = tmpd[:, 0:2 * DEPTH * 48].rearrange("p (t l k) -> p t l k", t=2, l=DEPTH)
        P.op("dve", lambda e: e.tensor_tensor(out=pr[:, 0], in0=lamv[:, :, 0, :], in1=lamv[:, :, 1, :], op=ALU.mult), [pF_b], [tmpd_b])
        P.op("dve", lambda e: e.tensor_tensor(out=pr[:, 1], in0=lamv[:, :, 2, :], in1=lamv[:, :, 3, :], op=ALU.mult), [pF_b, tmpd_b], [tmpd_b])
        sc = tmpd[:, 2 * DEPTH * 48:2 * DEPTH * 48 + 64]
        P.op("dve", lambda e: e.tensor_reduce(out=sc[:, 0:2 * DEPTH], in_=tmpd[:, 0:2 * DEPTH * 48].rearrange("p (a k) -> p a k", k=48), axis=AX.X, op=ALU.add), [tmpd_b], [tmpd_b])
        P.op("act", lambda e: e.activation(out=sc[:, 8:8 + 2 * DEPTH], in_=sc[:, 0:2 * DEPTH], func=AF.Exp), [tmpd_b], [tmpd_b])
        P.op("dve", lambda e: e.tensor_tensor(out=der[:, 0:DEPTH], in0=sc[:, 8:8 + DEPTH], in1=sc[:, 8 + DEPTH:8 + 2 * DEPTH], op=ALU.subtract), [tmpd_b], [der_b])
        for l in range(DEPTH):
            P.op("dve", lambda e, l=l: e.tensor_scalar(out=der[:, l:l + 1], in0=der[:, l:l + 1], scalar1=float(LAM_INIT[l]), scalar2=None, op0=ALU.add), [der_b], [der_b])
        P.op("dve", lambda e: e.tensor_scalar(out=der[:, 4:8], in0=der[:, 0:4], scalar1=-1.0, scalar2=None, op0=ALU.mult), [der_b], [der_b])
        lg = pf("clb")
        w0 = sc[:, 16:24].rearrange("p (a l) -> p a l", a=2)
        w1 = sc[:, 24:32].rearrange("p (a l) -> p a l", a=2)
        mx = sc[:, 32:34]
        P.op("dve", lambda e: e.tensor_reduce(out=mx, in_=lg, axis=AX.X, op=ALU.max), [pF_b, tmpd_b], [tmpd_b])
        P.op("dve", lambda e: e.tensor_tensor(out=w0, in0=lg, in1=bcast(mx, 2, [128, 2, DEPTH]), op=ALU.subtract), [pF_b, tmpd_b], [tmpd_b])
        P.op("act", lambda e: e.activation(out=w1, in_=w0, func=AF.Exp), [tmpd_b], [tmpd_b])
        sm = sc[:, 34:36]
        P.op("dve", lambda e: e.tensor_reduce(out=sm, in_=w1, axis=AX.X, op=ALU.add), [tmpd_b], [tmpd_b])
        P.op("dve", lambda e: e.reciprocal(out=sm, in_=sm), [tmpd_b], [tmpd_b])
        P.op("dve", lambda e: e.tensor_tensor(out=w1, in0=w1, in1=bcast(sm, 2, [128, 2, DEPTH]), op=ALU.mult), [tmpd_b], [tmpd_b])
        lbv = der[:, 8:16].rearrange("p (a l) -> p a l", a=2)
        omv = der[:, 16:24].rearrange("p (a l) -> p a l", a=2)
        P.op("dve", lambda e: e.memset(lbv[:, :, 0:1], 0.0), [der_b], [der_b])
        for l in range(1, DEPTH):
            P.op("dve", lambda e, l=l: e.tensor_tensor(out=lbv[:, :, l:l + 1], in0=lbv[:, :, l - 1:l], in1=w1[:, :, l:l + 1], op=ALU.add), [tmpd_b, der_b], [der_b])
        P.op("dve", lambda e: e.tensor_scalar(out=lbv, in0=lbv, scalar1=0.0, scalar2=None, op0=ALU.max), [der_b], [der_b])
        P.op("dve", lambda e: e.tensor_scalar(out=omv, in0=lbv, scalar1=-1.0, scalar2=1.0, op0=ALU.mult, op1=ALU.add), [der_b], [der_b])
        gsub = der[:, 24:24 + 4 * DEPTH].rearrange("p (l h) -> p l h", l=DEPTH)
        for l in range(DEPTH):
            P.op("dve", lambda e, l=l: e.tensor_scalar(out=gsub[:, l, :], in0=pf("bsub")[:, l, :], scalar1=float(1.0 - LAM_INIT[l]), scalar2=None, op0=ALU.mult), [pF_b, der_b], [der_b])

        s_x = P.slot("x")
        s_o = P.slot("o")
        wslots = {k: [P.slot(f"{k}{i}") for i in range(n)] for k, n in (("wA", 1), ("wB", 1), ("wC", 1), ("wO", 1), ("wU", 2), ("wD", 2))}

        def rmsnorm_T(gname, l, hT, hTb, scope):
            sq = sb("n_sq", [128, 3, 512], BF16, scope)
            sqb = mkbufs("n_sq", 3)
            tt = sb("n_t", [128, 2, 512], F32, scope)
            ttb = mkbufs("n_t", 2)
            g = pf(gname)
            for tb in range(4):
                cs = slice(tb * 512, (tb + 1) * 512)
                pb = 6 + (tb % 2)
                for c in range(8):
                    r = c % 3
                    P.op("act", lambda e, c=c, r=r: e.activation(out=sq[:, r, :], in_=X[:, c, cs], func=AF.Square), [Xb[c][tb]], [sqb[r]])
                    P.op("pe", lambda e, c=c, r=r: e.matmul(PS[pb][:, :], lhsT=ones, rhs=sq[:, r, :], start=(c == 0), stop=(c == 7)), [sqb[r], cB_b], [PSb[pb]])
                k = tb % 2
                P.op("act", lambda e: e.activation(out=tt[:, k, :], in_=PS[pb][:, :], func=AF.Ln, scale=1.0 / D, bias=cb_eps(1)), [PSb[pb], pF_b], [ttb[k]])
                P.op("act", lambda e: e.activation(out=tt[:, k, :], in_=tt[:, k, :], func=AF.Exp, scale=-0.5), [ttb[k]], [ttb[k]])
                for c in range(8):
                    P.op("dve", lambda e, c=c: e.scalar_tensor_tensor(out=hT[:, c, cs], in0=X[:, c, cs], scalar=g[:, l, c:c + 1], in1=tt[:, k, :], op0=ALU.mult, op1=ALU.mult),
                         [Xb[c][tb], ttb[k], pF_b], [hTb[c][tb]])

        def cb_eps(mult):
            idx = {1: 0, 64: 1, 48: 2}[mult]
            return pf("epsv")[:, idx:idx + 1]

        def load_w(slot, dst, dst_b, src, kcs):
            n = src.shape[-1]
            step = max(1, 2048 // n)
            P.dma_group("pool", slot, [(dst[:, k0:min(kcs, k0 + step), :], src[:, k0:min(kcs, k0 + step), :]) for k0 in range(0, kcs, step)], writes=dst_b)

        def job_A(l, g, hT, hTb, mix, mixb, scope):
            W = sb("a_w", [128, 8, 320], BF16, scope); Wb = mkbufs("a_w", 3)
            load_w(wslots["wA"][0], W, Wb, wA_d[l, g], 8)
            QT = sb("a_qt", [128, 4, S], BF16, scope); QTb = mkbufs("a_qt", 16)
            Vp = sb("a_vp", [128, 16, 128], BF16, scope); Vpb = mkbufs("a_vp", 16)
            Vp1 = Buf("a_vp1")
            P.op("pool", lambda e: e.memset(Vp[:, :, 64:128], 1.0), [], [Vp1])
            NR = 3
            from contextlib import ExitStack as _ES
            sci = scope.enter_context(_ES())
            sq = sb("a_sq", [128, NR, 256], F32, sci); sqb = mkbufs("a_sq", NR)
            st = sb("a_st", [128, NR, 16], F32, sci); stb = mkbufs("a_st", NR)
            t1 = sb("a_t1", [128, NR, 256], F32, sci); t1b = mkbufs("a_t1", NR)
            t2 = sb("a_t2", [128, NR, 256], F32, sci); t2b = mkbufs("a_t2", NR)
            rp = sb("a_rp", [128, NR, 4, 128], F32, sci); rpb = mkbufs("a_rp", NR)
            rot = sb("a_rot", [128, NR, 448], BF16, sci); rotb = mkbufs("a_rot", NR)
            P.op("pool", lambda e: e.memset(rot[:, :, 256:384], 0.0), [], rotb)
            gA = pf("gA")
            cs_ = pf("cos")
            sn_ = pf("sin")

            def emit_P(i):
                pb = 3 + i % NR
                ts = slice(i * 128, (i + 1) * 128)
                for c in range(8):
                    P.op("pe", lambda e, c=c: e.matmul(PS[pb][:, 0:320], lhsT=hT[:, c, ts], rhs=W[:, c, :], start=(c == 0), stop=(c == 7)),
                         [hTb[c][i // 4]] + Wb, [PSb[pb]])

            def emit_E(i):
                r = i % NR
                pb = 3 + i % NR
                P.op("act", lambda e: e.activation(out=sq[:, r, :], in_=PS[pb][:, 0:256], func=AF.Square), [PSb[pb]], [sqb[r]])
                P.op("act", lambda e: e.activation(out=Vp[:, i, 0:64], in_=PS[pb][:, 256:320], func=AF.Copy), [PSb[pb]], [Vpb[i]])
                ss = st[:, r, 0:4]; tq = st[:, r, 4:8]
                P.op("dve", lambda e: e.tensor_reduce(out=ss, in_=sq[:, r, :].rearrange("p (h d) -> p h d", h=4), axis=AX.X, op=ALU.add), [sqb[r]], [stb[r]])
                P.op("act", lambda e: e.activation(out=tq[:, 0:3], in_=ss[:, 0:3], func=AF.Ln, scale=1.0, bias=cb_eps(64)), [stb[r], pF_b], [stb[r]])
                P.op("act", lambda e: e.activation(out=tq[:, 3:4], in_=ss[:, 3:4], func=AF.Ln, scale=1.0 / 64, bias=cb_eps(1)), [stb[r], pF_b], [stb[r]])
                P.op("act", lambda e: e.activation(out=tq, in_=tq, func=AF.Exp, scale=-0.5), [stb[r]], [stb[r]])
                P.op("dve", lambda e: e.tensor_tensor(out=t1[:, r, :].rearrange("p (h d) -> p h d", h=4), in0=PS[pb][:, 0:256].rearrange("p (h d) -> p h d", h=4),
                                                      in1=bcast(tq, 2, [128, 4, 64]), op=ALU.mult), [PSb[pb], stb[r]], [t1b[r]])
                P.op("pool", lambda e: e.tensor_tensor(out=t2[:, r, :], in0=t1[:, r, :], in1=gA[:, l, :], op=ALU.mult), [t1b[r], pF_b], [t2b[r]])
                v = t2[:, r, :].rearrange("p (g h j) -> p g h j", g=8, h=2)
                x1, x2 = v[:, :, 0, :], v[:, :, 1, :]
                cosb = bcast(cs_[:, i, :].rearrange("p (a j) -> p a j", a=2), 1, [128, 4, 2, 16])
                sinb = bcast(sn_[:, i, :].rearrange("p (a j) -> p a j", a=2), 1, [128, 4, 2, 16])
                x1v = x1.rearrange("p (h a) j -> p h a j", h=4)
                x2v = x2.rearrange("p (h a) j -> p h a j", h=4)
                pa = [rp[:, r, k, :].rearrange("p (h a j) -> p h a j", h=4, a=2) for k in range(4)]
                P.op("pool", lambda e: e.tensor_tensor(out=pa[0], in0=x1v, in1=cosb, op=ALU.mult), [t2b[r], pF_b], [rpb[r]])
                P.op("dve", lambda e: e.tensor_tensor(out=pa[1], in0=x2v, in1=sinb, op=ALU.mult), [t2b[r], pF_b], [rpb[r]])
                P.op("pool", lambda e: e.tensor_tensor(out=pa[2], in0=x2v, in1=cosb, op=ALU.mult), [t2b[r], pF_b], [rpb[r]])
                P.op("dve", lambda e: e.tensor_tensor(out=pa[3], in0=x1v, in1=sinb, op=ALU.mult), [t2b[r], pF_b], [rpb[r]])
                ro = rot[:, r, 0:256].rearrange("p (g h j) -> p g h j", g=8, h=2)
                P.op("dve", lambda e: e.tensor_tensor(out=ro[:, :, 0, :], in0=rp[:, r, 0, :].rearrange("p (g j) -> p g j", g=8), in1=rp[:, r, 1, :].rearrange("p (g j) -> p g j", g=8), op=ALU.subtract),
                     [rpb[r]], [rotb[r]])
                P.op("pool", lambda e: e.tensor_tensor(out=ro[:, :, 1, :], in0=rp[:, r, 2, :].rearrange("p (g j) -> p g j", g=8), in1=rp[:, r, 3, :].rearrange("p (g j) -> p g j", g=8), op=ALU.add),
                     [rpb[r]], [rotb[r]])
                P.op("pool", lambda e: e.tensor_copy(out=rot[:, r, 384:448], in_=rot[:, r, 192:256]), [rotb[r]], [rotb[r]])

            def emit_T(i):
                r = i % NR
                tk = 6 + i % 2
                ts = slice(i * 128, (i + 1) * 128)
                tp = PS[tk][:, :].bitcast(BF16)
                for k, c0 in enumerate((0, 128, 192, 320)):
                    P.op("pe", lambda e, k=k, c0=c0: e.transpose(tp[:, k * 128:(k + 1) * 128], rot[:, r, c0:c0 + 128], ident), [rotb[r], cB_b], [PSb[tk]])
                P.op("act", lambda e: e.activation(out=QT[:, :, ts], in_=tp[:, 0:512].rearrange("p (k t) -> p k t", k=4), func=AF.Copy), [PSb[tk]], [QTb[i]])

            emit_P(0)
            emit_P(1)
            emit_E(0)
            for i in range(16):
                if i + 2 < 16:
                    emit_P(i + 2)
                if i + 1 < 16:
                    emit_E(i + 1)
                emit_T(i)
            P.barrier()
            sci.close()
            PT = sb("a_pt", [128, 3, 1024], BF16, scope); PTb = mkbufs("a_pt", 3)
            rec = sb("a_rec", [128, 2, 512], F32, scope); recb = mkbufs("a_rec", 2)
            pairs = [(hq, qb, jp) for hq in range(3) for qb in range(4) for jp in range(8)]
            hq_loc = [(0, 2), (0, 3), (1, 2)]

            def emit_S(n):
                hq, qb, jp = pairs[n]
                ti, kt = hq_loc[hq]
                pr3 = n % 3
                for u in range(2):
                    j = 2 * jp + u
                    bk = 2 * pr3 + u
                    P.op("pe", lambda e, j=j, bk=bk: e.matmul(PS[bk][:, :], lhsT=QT[:, kt, j * 128:(j + 1) * 128], rhs=QT[:, ti, qb * 512:(qb + 1) * 512], start=True, stop=True),
                         [QTb[j]] + [QTb[qb * 4 + k] for k in range(4)], [PSb[bk]])
                P.op("act", lambda e: e.activation(out=PT[:, pr3, :], in_=PSALL[:, 1024 * pr3:1024 * (pr3 + 1)], func=AF.Exp), [PSb[2 * pr3], PSb[2 * pr3 + 1]], [PTb[pr3]])

            emit_S(0)
            for n, (hq, qb, jp) in enumerate(pairs):
                if n + 1 < len(pairs):
                    emit_S(n + 1)
                ob = 6 + (hq * 4 + qb) % 2
                pr3 = n % 3
                for u in range(2):
                    j = 2 * jp + u
                    P.op("pe", lambda e, j=j, u=u: e.matmul(PS[ob][:, :], lhsT=Vp[:, j, :], rhs=PT[:, pr3, u * 512:(u + 1) * 512], start=(j == 0), stop=(j == 15)), [Vpb[j], Vp1, PTb[pr3]], [PSb[ob]])
                if jp == 7:
                    H = 3 * g + hq
                    rr = (hq * 4 + qb) % 2
                    po = (H % 2) * 64
                    P.op("dve", lambda e: e.reciprocal(out=rec[64:128, rr, :], in_=PS[ob][64:128, :]), [PSb[ob]], [recb[rr]])
                    P.op("dve", lambda e: e.tensor_tensor(out=mix(H // 2)[po:po + 64, qb * 512:(qb + 1) * 512], in0=PS[ob][0:64, :], in1=rec[64:128, rr, :], op=ALU.mult),
                         [PSb[ob], recb[rr]], [mixb[H // 2][qb][H % 2]])

        def job_B(l, h, hT, hTb, mix, mixb, scope):
            W = sb("b_w", [128, 8, 288], BF16, scope); Wb = mkbufs("b_w", 3)
            load_w(wslots["wB"][0], W, Wb, wB_d[l, h], 8)
            QB = sb("b_qb", [128, 6, S], BF16, scope); QBb = mkbufs("b_qb", 16)
            P.op("pool", lambda e: e.memset(QB[64:128, 2:4, :], 0.0), [], QBb)
            P.op("pool", lambda e: e.memset(QB[0:64, 4:6, :], 0.0), [], QBb)
            Vp = sb("b_vp", [128, 16, 128], BF16, scope); Vpb = mkbufs("b_vp", 16)
            Vp1 = Buf("b_vp1")
            P.op("pool", lambda e: e.memset(Vp[:, :, 96:128], 1.0), [], [Vp1])
            from contextlib import ExitStack as _ES
            sci = scope.enter_context(_ES())
            NR = 3
            sq = sb("b_sq", [128, NR, 192], F32, sci); sqb = mkbufs("b_sq", NR)
            st = sb("b_st", [128, NR, 16], F32, sci); stb = mkbufs("b_st", NR)
            t1 = sb("b_t1", [128, NR, 192], F32, sci); t1b = mkbufs("b_t1", NR)
            stg = sb("b_stg", [128, NR, 512], BF16, sci); stgb = mkbufs("b_stg", NR)
            P.op("pool", lambda e: e.memset(stg[:, :, :], 0.0), [], stgb)
            gB = pf("gB")
            aug = cb("aug")

            def emit_P(i):
                pb = 3 + i % NR
                ts = slice(i * 128, (i + 1) * 128)
                for c in range(8):
                    P.op("pe", lambda e, c=c: e.matmul(PS[pb][:, 0:288], lhsT=hT[:, c, ts], rhs=W[:, c, :], start=(c == 0), stop=(c == 7)),
                         [hTb[c][i // 4]] + Wb, [PSb[pb]])

            def emit_E(i):
                r = i % NR
                pb = 3 + i % NR
                P.op("act", lambda e: e.activation(out=sq[:, r, :], in_=PS[pb][:, 0:192], func=AF.Square), [PSb[pb]], [sqb[r]])
                P.op("act", lambda e: e.activation(out=Vp[:, i, 0:96], in_=PS[pb][:, 192:288], func=AF.Copy), [PSb[pb]], [Vpb[i]])
                ss = st[:, r, 0:4]; tq = st[:, r, 4:8]
                P.op("dve", lambda e: e.tensor_reduce(out=ss, in_=sq[:, r, :].rearrange("p (h d) -> p h d", h=4), axis=AX.X, op=ALU.add), [sqb[r]], [stb[r]])
                P.op("act", lambda e: e.activation(out=tq[:, 0:2], in_=ss[:, 0:2], func=AF.Ln, scale=1.0, bias=cb_eps(48)), [stb[r], pF_b], [stb[r]])
                P.op("act", lambda e: e.activation(out=tq[:, 2:4], in_=ss[:, 2:4], func=AF.Ln, scale=1.0 / 48, bias=cb_eps(1)), [stb[r], pF_b], [stb[r]])
                P.op("act", lambda e: e.activation(out=tq, in_=tq, func=AF.Exp, scale=-0.5), [stb[r]], [stb[r]])
                P.op("dve", lambda e: e.tensor_tensor(out=t1[:, r, :].rearrange("p (h d) -> p h d", h=4), in0=PS[pb][:, 0:192].rearrange("p (h d) -> p h d", h=4),
                                                      in1=bcast(tq, 2, [128, 4, 48]), op=ALU.mult), [PSb[pb], stb[r]], [t1b[r]])
                sv = stg[:, r, :].rearrange("p (g u c) -> p g u c", g=4, u=2)
                P.op("pool", lambda e: e.tensor_tensor(out=sv[:, :, 0, 0:48], in0=t1[:, r, :].rearrange("p (h d) -> p h d", h=4),
                                                       in1=gB[:, l, :].rearrange("p (h d) -> p h d", h=4), op=ALU.mult), [t1b[r], pF_b], [stgb[r]])
                P.op("dve", lambda e: e.tensor_tensor(out=sv[:, :, 1, 0:48], in0=t1[:, r, :].rearrange("p (h d) -> p h d", h=4),
                                                      in1=gB[:, l, :].rearrange("p (h d) -> p h d", h=4), op=ALU.mult), [t1b[r], pF_b], [stgb[r]])
                P.op("pool", lambda e: e.tensor_copy(out=sv[:, :, :, 48:52], in_=aug[:, h, i, :].rearrange("p (g u c) -> p g u c", g=4, u=2)), [cB_b], [stgb[r]])

            def emit_T(i):
                r = i % NR
                tk = 6 + i % 2
                ts = slice(i * 128, (i + 1) * 128)
                tp = PS[tk][:, :].bitcast(BF16)
                for k in range(4):
                    P.op("pe", lambda e, k=k: e.transpose(tp[:, k * 128:(k + 1) * 128], stg[:, r, k * 128:(k + 1) * 128], ident), [stgb[r], cB_b], [PSb[tk]])
                P.op("act", lambda e: e.activation(out=QB[:, 0:2, ts], in_=tp[:, 0:256].rearrange("p (k t) -> p k t", k=2), func=AF.Copy), [PSb[tk]], [QBb[i]])
                P.op("dve", lambda e: e.tensor_copy(out=QB[0:64, 2:4, ts], in_=tp[0:64, 256:512].rearrange("p (k t) -> p k t", k=2)), [PSb[tk]], [QBb[i]])
                P.op("act", lambda e: e.activation(out=QB[64:128, 4:6, ts], in_=tp[64:128, 256:512].rearrange("p (k t) -> p k t", k=2), func=AF.Copy), [PSb[tk]], [QBb[i]])

            emit_P(0)
            emit_P(1)
            emit_E(0)
            for i in range(16):
                if i + 2 < 16:
                    emit_P(i + 2)
                if i + 1 < 16:
                    emit_E(i + 1)
                emit_T(i)
            P.barrier()
            sci.close()
            PT = sb("b_pt", [128, 3, 1024], BF16, scope); PTb = mkbufs("b_pt", 3)
            rc = sb("b_rc", [128, 1, 512], F32, scope); rcb = mkbufs("b_rc", 1) * 2
            vv = sb("b_vv", [128, 2, 512], F32, scope); vvb = mkbufs("b_vv", 2)
            sqv = sb("b_sqv", [128, 512], BF16, scope); sqvb = Buf("b_sqv")
            tv = sb("b_tv", [128, 512], F32, scope); tvb = Buf("b_tv")
            dg = cb("dg")
            pairs = [(qb, cp, jp) for qb in range(4) for cp in range(2) for jp in range(8)]

            def emit_S(n):
                qb, cp, jp = pairs[n]
                qg, kp, km = cp, 2 + cp, 4 + cp
                pr3 = n % 3
                q0 = qb * 512
                for u in range(2):
                    j = 2 * jp + u
                    bk = 2 * pr3 + u
                    rel = j - 4 * qb
                    kc = slice(j * 128, (j + 1) * 128)
                    rd = [QBb[j]] + [QBb[qb * 4 + k] for k in range(4)]
                    if rel < 0:
                        P.op("pe", lambda e: e.matmul(PS[bk][:, :], lhsT=QB[:, kp, kc], rhs=QB[:, qg, q0:q0 + 512], start=True, stop=True), rd, [PSb[bk]])
                    elif rel > 3:
                        P.op("pe", lambda e: e.matmul(PS[bk][:, :], lhsT=QB[:, km, kc], rhs=QB[:, qg, q0:q0 + 512], start=True, stop=True), rd, [PSb[bk]])
                    else:
                        m = rel
                        mms = []
                        if m > 0:
                            mms.append((PS[bk][:, 0:128 * m], QB[:, km, kc], QB[:, qg, q0:q0 + 128 * m], rd))
                        mms.append((PS[bk][:, 128 * m:512], QB[:, kp, kc], QB[:, qg, q0 + 128 * m:q0 + 512], rd))
                        mms.append((PS[bk][:, 128 * m:128 * m + 128], ident, dg[:, h, :], [cB_b]))
                        for k_, (o_, l_, r_, rd_) in enumerate(mms):
                            P.op("pe", lambda e, o_=o_, l_=l_, r_=r_, k_=k_: e.matmul(o_, lhsT=l_, rhs=r_, start=(k_ == 0), stop=(k_ == len(mms) - 1)), rd_, [PSb[bk]])
                P.op("act", lambda e: e.activation(out=PT[:, pr3, :], in_=PSALL[:, 1024 * pr3:1024 * (pr3 + 1)], func=AF.Exp), [PSb[2 * pr3], PSb[2 * pr3 + 1]], [PTb[pr3]])

            pending = []
            emit_S(0)
            for n, (qb, cp, jp) in enumerate(pairs):
                if n + 1 < len(pairs):
                    emit_S(n + 1)
                ob = 6 + cp
                pr3 = n % 3
                for u in range(2):
                    j = 2 * jp + u
                    P.op("pe", lambda e, j=j, u=u: e.matmul(PS[ob][:, :], lhsT=Vp[:, j, :], rhs=PT[:, pr3, u * 512:(u + 1) * 512], start=(j == 0), stop=(j == 15)), [Vpb[j], Vp1, PTb[pr3]], [PSb[ob]])
                if jp == 7:
                    P.op("dve", lambda e: e.reciprocal(out=rc[96:128, 0, :], in_=PS[ob][96:128, :]), [PSb[ob]], [rcb[cp]])
                    for g3 in range(3):
                        P.op("dve", lambda e, g3=g3: e.tensor_tensor(out=vv[32 * g3:32 * g3 + 32, cp, :], in0=PS[ob][32 * g3:32 * g3 + 32, :], in1=rc[96:128, 0, :], op=ALU.mult),
                             [PSb[ob], rcb[cp]], [vvb[cp]])
                    if cp == 1:
                        P.op("dve", lambda e: e.scalar_tensor_tensor(out=vv[0:96, 0, :], in0=vv[0:96, 1, :], scalar=der[0:96, 4 + l:5 + l], in1=vv[0:96, 0, :], op0=ALU.mult, op1=ALU.add),
                             [vvb[0], vvb[1], der_b], [vvb[0]])

                        def tail2(qb=qb):
                            P.op("act", lambda e: e.activation(out=sqv[0:96, :], in_=vv[0:96, 0, :], func=AF.Square), [vvb[0]], [sqvb])
                            P.op("pe", lambda e: e.matmul(PS[7][0:96, :], lhsT=ones[0:96, 0:96], rhs=sqv[0:96, :], start=True, stop=True), [sqvb, cB_b], [PSb[7]])
                            P.op("act", lambda e: e.activation(out=tv[0:96, :], in_=PS[7][0:96, :], func=AF.Ln, scale=1.0 / 96, bias=cb_eps(1)[0:96]), [PSb[7], pF_b], [tvb])
                            P.op("act", lambda e: e.activation(out=tv[0:96, :], in_=tv[0:96, :], func=AF.Exp, scale=-0.5), [tvb], [tvb])
                            P.op("dve", lambda e: e.scalar_tensor_tensor(out=mix(3 + h)[0:96, qb * 512:(qb + 1) * 512], in0=vv[0:96, 0, :], scalar=gsub[0:96, l, h:h + 1], in1=tv[0:96, :],
                                                                         op0=ALU.mult, op1=ALU.mult), [vvb[0], tvb, der_b], [mixb[3 + h][qb][0]])
                        pending.append((n + 4, tail2))
                while pending and pending[0][0] <= n:
                    pending.pop(0)[1]()
            while pending:
                pending.pop(0)[1]()

        def job_C(l, pr_, hT, hTb, mix, mixb, scope):
            from contextlib import ExitStack as _ES
            QK = sb("c_qk", [128, 4, S], BF16, scope); QKb = mkbufs("c_qk", 4, 4)
            GS = sb("c_gs", [128, S], BF16, scope); GSb = mkbufs("c_gs", 4)
            KV = sb("c_kv", [128, 2, 4096], BF16, scope); KVb = mkbufs("c_kv", 2)
            Dn = sb("c_dn", [128, 2, 64], F32, scope); Dnb = mkbufs("c_dn", 2)
            Vt = sb("c_vt", [128, 16, 128], BF16, scope); Vtb = mkbufs("c_vt", 16)
            KVv = KV[:, :, :].rearrange("p d (v n) -> p d v n", n=64)
            lb_ap = der[:, 8 + pr_ * DEPTH + l:8 + pr_ * DEPTH + l + 1]
            om_ap = der[:, 16 + pr_ * DEPTH + l:16 + pr_ * DEPTH + l + 1]
            cm = cb("cmask")
            bm = cb("blkm")
            sc = scope.enter_context(_ES())
            W = sb("c_w", [128, 8, 640], BF16, sc); Wb = mkbufs("c_w", 3)
            load_w(wslots["wC"][0], W, Wb, wC_d[l, pr_], 8)
            Vk = sb("c_vk", [128, 2, 512], BF16, sc); Vkb = mkbufs("c_vk", 2)
            NT = 9
            tm = sb("c_tm", [128, NT, 512], F32, sc); tmb = mkbufs("c_tm", NT)
            khT = sb("c_kht", [128, 2, 512], BF16, sc); khTb = mkbufs("c_kht", 2)
            kh = sb("c_kh", [128, 2, 128], BF16, sc); khb = mkbufs("c_kh", 2)
            khn = 0
            for tb in range(4):
                cs = slice(tb * 512, (tb + 1) * 512)
                for grp in range(4):
                    for c in range(8):
                        P.op("pe", lambda e, c=c, grp=grp: e.matmul(PS[grp][:, :], lhsT=W[:, c, grp * 128:(grp + 1) * 128], rhs=hT[:, c, cs], start=(c == 0), stop=(c == 7)),
                             [hTb[c][tb]] + Wb, [PSb[grp]])
                for t4 in range(4):
                    i = tb * 4 + t4
                    pb = 4 + (i % 2)
                    ts = slice(i * 128, (i + 1) * 128)
                    for c in range(8):
                        P.op("pe", lambda e, c=c: e.matmul(PS[pb][:, 0:128], lhsT=hT[:, c, ts], rhs=W[:, c, 512:640], start=(c == 0), stop=(c == 7)), [hTb[c][tb]] + Wb, [PSb[pb]])
                    P.op("act", lambda e: e.activation(out=Vt[:, i, :], in_=PS[pb][:, 0:128], func=AF.Copy), [PSb[pb]], [Vtb[i]])
                qs, sf, sbw = 0, 1, 2
                P.op("act", lambda e: e.activation(out=tm[:, qs, :], in_=PS[0][:, :], func=AF.Silu), [PSb[0]], [tmb[qs]])
                P.op("act", lambda e: e.activation(out=GS[:, cs], in_=PS[3][:, :], func=AF.Silu), [PSb[3]], [GSb[tb]])
                P.op("act", lambda e: e.activation(out=tm[:, sf, :], in_=PS[1][:, :], func=AF.Sigmoid), [PSb[1]], [tmb[sf]])
                P.op("act", lambda e: e.activation(out=tm[:, sbw, :], in_=PS[2][:, :], func=AF.Sigmoid), [PSb[2]], [tmb[sbw]])
                def chain(d_):
                    f_ = sf if d_ == 0 else sbw
                    k_, lf_, b_ = (3, 4, 5) if d_ == 0 else (6, 7, 8)
                    t_, e_ = b_, f_
                    P.op("pool", lambda e: e.tensor_scalar(out=tm[:, f_, :], in0=tm[:, f_, :], scalar1=om_ap, scalar2=lb_ap, op0=ALU.mult, op1=ALU.add), [tmb[f_], der_b], [tmb[f_]]); yield
                    P.op("dve", lambda e: e.tensor_scalar(out=tm[:, f_, :], in0=tm[:, f_, :], scalar1=1e-20, scalar2=None, op0=ALU.max), [tmb[f_]], [tmb[f_]]); yield
                    P.op("pool", lambda e: e.tensor_scalar(out=tm[:, k_, :], in0=tm[:, f_, :], scalar1=-1.0, scalar2=1.0, op0=ALU.mult, op1=ALU.add), [tmb[f_]], [tmb[k_]])
                    P.op("act", lambda e: e.activation(out=tm[:, lf_, :], in_=tm[:, f_, :], func=AF.Ln), [tmb[f_]], [tmb[lf_]]); yield
                    if d_ == 0:
                        P.op("dve", lambda e: e.tensor_tensor_scan(out=tm[:, b_, :], data0=cm[:, 0, :], data1=tm[:, lf_, :], initial=0.0, op0=ALU.mult, op1=ALU.add), [tmb[lf_], cB_b], [tmb[b_]])
                    else:
                        P.op("dve", lambda e: e.tensor_tensor_scan(out=tm[:, b_, ::-1], data0=cm[:, 1, ::-1], data1=tm[:, lf_, ::-1], initial=0.0, op0=ALU.mult, op1=ALU.add),
                             [tmb[lf_], cB_b], [tmb[b_]])
                    yield
                    bv = tm[:, b_, :].rearrange("p (n t) -> p n t", t=32)
                    be = bv[:, :, 31:32] if d_ == 0 else bv[:, :, 0:1]
                    P.op("pool", lambda e: e.tensor_tensor(out=tm[:, lf_, :].rearrange("p (n t) -> p n t", t=32), in0=be.to_broadcast([128, 16, 32]), in1=bv, op=ALU.subtract), [tmb[b_], tmb[lf_]], [tmb[lf_]])
                    P.op("act", lambda e: e.activation(out=Dn[:, d_, tb * 16:(tb + 1) * 16], in_=be.rearrange("p n o -> p (n o)"), func=AF.Exp), [tmb[b_]], [Dnb[d_]]); yield
                    P.op("act", lambda e: e.activation(out=tm[:, lf_, :], in_=tm[:, lf_, :], func=AF.Exp), [tmb[lf_]], [tmb[lf_]])
                    P.op("dve", lambda e: e.tensor_scalar(out=tm[:, b_, :], in0=tm[:, b_, :], scalar1=-60.0, scalar2=None, op0=ALU.max), [tmb[b_]], [tmb[b_]]); yield
                    P.op("pool", lambda e: e.tensor_tensor(out=khT[:, d_, :], in0=tm[:, k_, :], in1=tm[:, lf_, :], op=ALU.mult), [tmb[k_], tmb[lf_]], [khTb[d_]])
                    P.op("act", lambda e: e.activation(out=tm[:, e_, :], in_=tm[:, b_, :], func=AF.Exp, scale=-1.0), [tmb[b_]], [tmb[e_]])
                    P.op("act", lambda e: e.activation(out=tm[:, t_, :], in_=tm[:, b_, :], func=AF.Exp), [tmb[b_]], [tmb[t_]]); yield
                    P.op("dve", lambda e: e.tensor_tensor(out=QK[:, 2 * d_, cs], in0=tm[:, qs, :], in1=tm[:, t_, :], op=ALU.mult), [tmb[qs], tmb[t_]], [QKb[2 * d_][tb]])
                    P.op("dve", lambda e: e.tensor_tensor(out=QK[:, 2 * d_ + 1, cs], in0=tm[:, k_, :], in1=tm[:, e_, :], op=ALU.mult), [tmb[k_], tmb[e_]], [QKb[2 * d_ + 1][tb]]); yield

                gens = [chain(0), chain(1)]
                while gens:
                    for g_ in list(gens):
                        try:
                            next(g_)
                        except StopIteration:
                            gens.remove(g_)
                for t4 in range(4):
                    i = tb * 4 + t4
                    vkk = i % 2
                    P.op("dve", lambda e: e.tensor_tensor(out=Vk[:, vkk, :].rearrange("p (n v) -> p n v", n=4), in0=bcast(Vt[:, i, :], 1, [128, 4, 128]), in1=bm, op=ALU.mult),
                         [Vtb[i], cB_b], [Vkb[vkk]])
                    for d_ in range(2):
                        pb = 6 + (khn % 2)
                        kr = khn % 2
                        khn += 1
                        tp = PS[pb][:, :].bitcast(BF16)
                        P.op("pe", lambda e: e.transpose(tp[:, 0:128], khT[:, d_, t4 * 128:(t4 + 1) * 128], ident), [khTb[d_], cB_b], [PSb[pb]])
                        P.op("act", lambda e: e.activation(out=kh[:, kr, :], in_=tp[:, 0:128], func=AF.Copy), [PSb[pb]], [khb[kr]])
                        P.op("pe", lambda e: e.matmul(PS[pb][:, :], lhsT=kh[:, kr, :], rhs=Vk[:, vkk, :], start=True, stop=True), [khb[kr], Vkb[vkk]], [PSb[pb]])
                        for hh in range(2):
                            hs = slice(64 * hh, 64 * hh + 64)
                            src = PS[pb][hs, :].rearrange("p (n v) -> p v n", n=4)[:, 64 * hh:64 * hh + 64, :]
                            P.op("dve", lambda e, hs=hs, src=src: e.tensor_copy(out=KVv[hs, d_, :, 4 * i:4 * i + 4], in_=src), [PSb[pb]], [KVb[d_]])
            P.barrier()
            sc.close()
            sc = scope.enter_context(_ES())
            Df = sb("c_df", [128, 4096], BF16, sc); Dfb = Buf("c_df")
            Dfv = Df[:, :].rearrange("p (v n) -> p v n", n=64)
            for d_ in range(2):
                P.op("act", lambda e: e.activation(out=Dfv, in_=bcast(Dn[:, d_, :], 1, [128, 64, 64]), func=AF.Copy), [Dnb[d_]], [Dfb])
                zc = 0 if d_ == 0 else 63
                P.op("pool", lambda e: e.memset(Dfv[:, :, zc:zc + 1], 0.0), [], [Dfb])
                if d_ == 0:
                    P.op("dve", lambda e: e.tensor_tensor_scan(out=KV[:, 0, :], data0=Df[:, :], data1=KV[:, 0, :], initial=0.0, op0=ALU.mult, op1=ALU.add), [Dfb, KVb[0]], [KVb[0]])
                else:
                    P.op("dve", lambda e: e.tensor_tensor_scan(out=KV[:, 1, ::-1], data0=Df[:, ::-1], data1=KV[:, 1, ::-1], initial=0.0, op0=ALU.mult, op1=ALU.add), [Dfb, KVb[1]], [KVb[1]])
            P.barrier()
            sc.close()
            Am = sb("c_am", [128, 2, 2, 256], BF16, scope); Amb = mkbufs("c_am", 2, 2)
            oc = sb("c_oc", [128, 2, 512], F32, scope); ocb = mkbufs("c_oc", 2)
            sqo = sb("c_sqo", [128, 512], BF16, scope); sqob = Buf("c_sqo")
            to = sb("c_to", [128, 2, 512], F32, scope); tob = mkbufs("c_to", 2)
            mk = cb("hmask")
            gco = pf("gco")
            def emit_A(i):
                ts = slice(i * 128, (i + 1) * 128)
                r = i % 2
                tb = i // 4
                for hh in range(2):
                    hs = slice(64 * hh, 64 * hh + 64)
                    ab = 2 * r + hh
                    for d_ in range(2):
                        P.op("pe", lambda e, d_=d_, hs=hs, ab=ab: e.matmul(PS[ab][:, d_ * 128:(d_ + 1) * 128], lhsT=QK[hs, 2 * d_ + 1, ts], rhs=QK[hs, 2 * d_, ts], start=True, stop=True),
                             [QKb[2 * d_ + 1][tb], QKb[2 * d_][tb]], [PSb[ab]])
                    P.op("dve", lambda e, hh=hh, ab=ab: e.tensor_tensor(out=Am[:, r, hh, :], in0=PS[ab][:, 0:256], in1=mk.rearrange("p d t -> p (d t)"), op=ALU.mult), [PSb[ab], cB_b], [Amb[r][hh]])

            def emit_O(i):
                ts = slice(i * 128, (i + 1) * 128)
                r = i % 2
                tb = i // 4
                for hh in range(2):
                    ob = 4 + r + 2 * hh
                    hs = slice(64 * hh, 64 * hh + 64)
                    for d_ in range(2):
                        P.op("pe", lambda e, d_=d_, hs=hs, hh=hh: e.matmul(PS[ob][hs, 0:128], lhsT=Vt[:, i, hs], rhs=Am[:, r, hh, d_ * 128:(d_ + 1) * 128], start=(d_ == 0), stop=False),
                             [Vtb[i], Amb[r][hh]], [PSb[ob]])
                    inter = [(d_, n4, (4 * i + n4 - 1) if d_ == 0 else (4 * i + n4 + 1)) for d_ in range(2) for n4 in range(4)]
                    inter = [(d_, n4, idx) for (d_, n4, idx) in inter if 0 <= idx <= 63]
                    for k_, (d_, n4, idx) in enumerate(inter):
                        P.op("pe", lambda e, d_=d_, hs=hs, n4=n4, idx=idx, k_=k_: e.matmul(PS[ob][hs, 32 * n4:32 * n4 + 32], lhsT=KVv[hs, d_, :, idx], rhs=QK[hs, 2 * d_, i * 128 + 32 * n4:i * 128 + 32 * n4 + 32],
                                                                                           start=False, stop=(k_ == len(inter) - 1)), [KVb[d_], QKb[2 * d_][tb]], [PSb[ob]])
                t4 = i % 4
                rr = tb % 2
                for hh in range(2):
                    hs = slice(64 * hh, 64 * hh + 64)
                    P.op("act", lambda e, hs=hs, hh=hh: e.activation(out=oc[hs, rr, t4 * 128:(t4 + 1) * 128], in_=PS[4 + r + 2 * hh][hs, 0:128], func=AF.Copy), [PSb[4 + r + 2 * hh]], [ocb[rr]])
                if t4 == 3:
                    cs = slice(tb * 512, (tb + 1) * 512)
                    P.op("act", lambda e: e.activation(out=sqo[:, :], in_=oc[:, rr, :], func=AF.Square), [ocb[rr]], [sqob])
                    P.op("pe", lambda e: e.matmul(PS[0][:, :], lhsT=blk1, rhs=sqo[:, :], start=True, stop=True), [sqob, cB_b], [PSb[0]])
                    P.op("act", lambda e: e.activation(out=to[:, rr, :], in_=PS[0][:, :], func=AF.Ln, scale=1.0 / 64, bias=cb_eps(1)), [PSb[0], pF_b], [tob[rr]])
                    P.op("act", lambda e: e.activation(out=to[:, rr, :], in_=to[:, rr, :], func=AF.Exp, scale=-0.5), [tob[rr]], [tob[rr]])
                    P.op("dve", lambda e: e.scalar_tensor_tensor(out=to[:, rr, :], in0=oc[:, rr, :], scalar=gco[:, l, pr_:pr_ + 1], in1=to[:, rr, :], op0=ALU.mult, op1=ALU.mult),
                         [ocb[rr], tob[rr], pF_b], [tob[rr]])
                    P.op("dve", lambda e: e.tensor_tensor(out=mix(7 + pr_)[:, cs], in0=to[:, rr, :], in1=GS[:, cs], op=ALU.mult), [tob[rr], GSb[tb]], [mixb[7 + pr_][tb][0]])


            for i in range(16):
                emit_A(i)
                emit_O(i)

        def out_proj(l, mix, mixb, chunks, scope):
            WO = sb("o_w", [128, 9, 1024], BF16, scope); WOb = mkbufs("o_w", 9)
            P.dma_group("pool", wslots["wO"][0], [(WO[:, ch, :], wO_d[l, ch]) for ch in chunks], writes=[WOb[ch] for ch in chunks])
            for tb in range(4):
                cs = slice(tb * 512, (tb + 1) * 512)
                for oc_ in range(8):
                    pb = oc_ % 4
                    for n, ch in enumerate(chunks):
                        K = 96 if 3 <= ch <= 6 else 128
                        P.op("pe", lambda e, ch=ch, K=K, n=n: e.matmul(PS[pb][:, :], lhsT=WO[0:K, ch, oc_ * 128:(oc_ + 1) * 128], rhs=mix(ch)[0:K, cs], start=(n == 0), stop=(n == len(chunks) - 1)),
                             [WOb[ch]] + mixb[ch][tb], [PSb[pb]])
                    P.op("dve", lambda e: e.tensor_tensor(out=X[:, oc_, cs], in0=PS[pb][:, :], in1=X[:, oc_, cs], op=ALU.add), [PSb[pb], Xb[oc_][tb]], [Xb[oc_][tb]])

        def attention_phase(l):
            with ExitStack() as sc1:
                hT = sb("hT", [128, 8, S], BF16, sc1); hTb = mkbufs("hT", 8, 4)
                mixC = sb("mixC", [128, 2, S], BF16, sc1)
                mixb = [[[Buf(f"mix{c}_{t}_{u}") for u in range(2)] for t in range(4)] for c in range(9)]
                holder = {}

                def mix(ch):
                    if ch >= 7:
                        return mixC[:, ch - 7, :]
                    return holder["ab"][:, ch, :]

                with ExitStack() as sc2:
                    rmsnorm_T("gattn", l, hT, hTb, sc2)
                    P.barrier()
                chunks = []
                if "C" in stages:
                    for pr_ in range(2):
                        with ExitStack() as sc2:
                            job_C(l, pr_, hT, hTb, mix, mixb, sc2)
                            P.barrier()
                    chunks += [7, 8]
                holder["ab"] = sb("mixAB", [128, 7, S], BF16, sc1)
                if "A" in stages:
                    for g in range(2):
                        with ExitStack() as sc2:
                            job_A(l, g, hT, hTb, mix, mixb, sc2)
                            P.barrier()
                    chunks += [0, 1, 2]
                if "B" in stages:
                    for h in range(4):
                        with ExitStack() as sc2:
                            job_B(l, h, hT, hTb, mix, mixb, sc2)
                            P.barrier()
                    chunks += [3, 4, 5, 6]
                if chunks:
                    with ExitStack() as sc2:
                        out_proj(l, mix, mixb, sorted(chunks), sc2)
                        P.barrier()

        def ffn_phase(l):
            with ExitStack() as sc1:
                hT = sb("h2T", [128, 8, S], BF16, sc1); hTb = mkbufs("h2T", 8, 4)
                with ExitStack() as sc2:
                    rmsnorm_T("gffn", l, hT, hTb, sc2)
                    P.barrier()
                U = sb("f_u", [128, NJ, 1024], BF16, sc1); Ub = mkbufs("f_u", NJ, 2)
                WU = sb("f_wu", [128, 2, 8, 256], BF16, sc1); WUb = mkbufs("f_wu", 2, 1)
                WD = sb("f_wd", [128, 2, NJ, 128], BF16, sc1); WDb = mkbufs("f_wd", 2, 2)
                gp = sb("f_gp", [128, 2, 1026], F32, sc1); gpb = mkbufs("f_gp", 2)
                cv = sb("f_cv", [128, 2, 1024], F32, sc1); cvb = mkbufs("f_cv", 2)
                P.op("pool", lambda e: e.memset(gp[:, :, :], 0.0), [], gpb)
                cw = pf("convw")
                cbv = pf("convb")
                nload = 0
                for half in range(2):
                    h0 = half * 1024
                    g0 = 0 if half == 0 else 1023
                    c0 = 1 if half == 0 else 0
                    zc = 0 if half == 0 else 1025
                    for gi in range(2):
                        P.op("pool", lambda e, gi=gi: e.memset(gp[:, gi, zc:zc + 1], 0.0), [], [gpb[gi]])
                    for j in range(NJ):
                        ws = nload % 2
                        nload += 1
                        load_w(wslots["wU"][ws], WU[:, ws], WUb[ws], wU_d[l, j], 8)
                        gr = j % 2
                        blocks = [(0, 342), (342, 342), (684, 341)]
                        for bi, (o, nn) in enumerate(blocks):
                            for c in range(8):
                                P.op("pe", lambda e, c=c, o=o, nn=nn, bi=bi: e.matmul(PS[bi][:, 0:nn], lhsT=WU[:, ws, c, 0:128], rhs=hT[:, c, g0 + o:g0 + o + nn], start=(c == 0), stop=(c == 7)),
                                     WUb[ws] + [hTb[c][t] for t in range(4)], [PSb[bi]])
                            P.op("act", lambda e, o=o, nn=nn, bi=bi: e.activation(out=gp[:, gr, c0 + o:c0 + o + nn], in_=PS[bi][:, 0:nn], func=AF.Copy), [PSb[bi]], [gpb[gr]])
                        for vb in range(2):
                            pb = 3 + vb + 2 * (j % 2)
                            for c in range(8):
                                P.op("pe", lambda e, c=c, vb=vb, pb=pb: e.matmul(PS[pb][:, :], lhsT=WU[:, ws, c, 128:256], rhs=hT[:, c, h0 + vb * 512:h0 + (vb + 1) * 512], start=(c == 0), stop=(c == 7)),
                                     WUb[ws] + [hTb[c][half * 2 + vb]], [PSb[pb]])
                        P.op("act", lambda e: e.activation(out=cv[:, gr, :], in_=gp[:, gr, 1:1025], func=AF.Identity, scale=cw[:, l, j, 1:2], bias=cbv[:, l, j:j + 1]), [gpb[gr], pF_b], [cvb[gr]])
                        P.op("dve", lambda e: e.scalar_tensor_tensor(out=cv[:, gr, :], in0=gp[:, gr, 0:1024], scalar=cw[:, l, j, 0:1], in1=cv[:, gr, :], op0=ALU.mult, op1=ALU.add), [gpb[gr], cvb[gr], pF_b], [cvb[gr]])
                        P.op("dve", lambda e: e.scalar_tensor_tensor(out=cv[:, gr, :], in0=gp[:, gr, 2:1026], scalar=cw[:, l, j, 2:3], in1=cv[:, gr, :], op0=ALU.mult, op1=ALU.add), [gpb[gr], cvb[gr], pF_b], [cvb[gr]])
                        P.op("act", lambda e: e.activation(out=cv[:, gr, :], in_=cv[:, gr, :], func=AF.Gelu), [cvb[gr]], [cvb[gr]])
                        for vb in range(2):
                            pb = 3 + vb + 2 * (j % 2)
                            P.op("dve", lambda e, vb=vb, pb=pb: e.tensor_tensor(out=U[:, j, vb * 512:(vb + 1) * 512], in0=PS[pb][:, :], in1=cv[:, gr, vb * 512:(vb + 1) * 512], op=ALU.mult),
                                 [PSb[pb], cvb[gr]], [Ub[j][vb]])
                    for oc_ in range(8):
                        ws = oc_ % 2
                        P.dma_group("pool", wslots["wD"][ws], [(WD[:, ws, j0:j0 + 11, :], wD_d[l, oc_, :, j0:j0 + 11, :]) for j0 in range(0, NJ, 11)], writes=WDb[ws])
                        for vb in range(2):
                            pb = (oc_ * 2 + vb) % 4
                            tb = half * 2 + vb
                            for j in range(NJ):
                                P.op("pe", lambda e, j=j, vb=vb, pb=pb: e.matmul(PS[pb][:, :], lhsT=WD[:, ws, j, :], rhs=U[:, j, vb * 512:(vb + 1) * 512], start=(j == 0), stop=(j == NJ - 1)),
                                     WDb[ws] + [Ub[j][vb]], [PSb[pb]])
                            P.op("dve", lambda e, vb=vb, pb=pb, tb=tb: e.tensor_tensor(out=X[:, oc_, tb * 512:(tb + 1) * 512], in0=PS[pb][:, :], in1=X[:, oc_, tb * 512:(tb + 1) * 512], op=ALU.add),
                                 [PSb[pb], Xb[oc_][tb]], [Xb[oc_][tb]])
                P.barrier()

        allX = [Xb[c][t] for c in range(8) for t in range(4)]
        for s in range(nseq):
            P.dma_group("sp", s_x, [(X[:, c, :], xT_d[s, c]) for c in range(8)], writes=allX)
            for l in range(depth):
                if any(k in stages for k in "ABC"):
                    attention_phase(l)
                if "F" in stages:
                    ffn_phase(l)
            P.dma_group("sp", s_o, [(out_d[s, c], X[:, c, :]) for c in range(8)], reads=allX)
        P.barrier()
        print(f"[build] ops={P.nop} waits={P.nwait} cnt={P.cnt}")
    return nc


def _layout(items):
    off = {}
    o = 0
    for name, shp in items:
        off[name] = (o, tuple(shp))
        o += int(np.prod(shp))
    return off, o


PF_ITEMS = [
    ("gattn", (DEPTH, 8)), ("gffn", (DEPTH, 8)), ("gA", (DEPTH, 256)), ("gB", (DEPTH, 192)), ("bsub", (DEPTH, 4)),
    ("blam", (DEPTH, 4, 48)), ("clb", (2, DEPTH)), ("gco", (DEPTH, 2)), ("convw", (DEPTH, NJ, 3)), ("convb", (DEPTH, NJ)),
    ("cos", (16, 32)), ("sin", (16, 32)), ("sel", (96,)), ("epsv", (4,)),
]
PF_OFF, PF_COLS = _layout(PF_ITEMS)
CB_ITEMS = [("ident", (128,)), ("ones", (128,)), ("blk1", (128,)), ("dg", (4, 128)), ("cmask", (2, 512)), ("blkm", (4, 128)), ("hmask", (2, 128)), ("aug", (4, 16, 32))]
CB_OFF, CB_COLS = _layout(CB_ITEMS)


def _const_tables():
    p = np.arange(128)
    cbt = np.zeros((128, CB_COLS), np.float32)

    def put(name, arr):
        o, shp = CB_OFF[name]
        cbt[:, o:o + int(np.prod(shp))] = np.asarray(arr, np.float32).reshape(128, -1)

    put("ident", np.eye(128))
    put("ones", np.ones((128, 128)))
    put("blk1", (p[:, None] // 64 == p[None, :] // 64))
    put("dg", np.stack([-2.0 * SLOPES[h] * np.maximum(p[:, None] - p[None, :], 0) for h in range(4)], axis=1))
    t = np.arange(512)
    cm = np.stack([(t % 32 != 0), (t % 32 != 31)]).astype(np.float32)
    put("cmask", np.broadcast_to(cm[None], (128, 2, 512)))
    put("blkm", np.stack([np.broadcast_to((p // 32 == n)[:, None], (128, 128)) for n in range(4)], axis=1))
    same = (p[:, None] // 32 == p[None, :] // 32)
    put("hmask", np.stack([same & (p[:, None] <= p[None, :]), same & (p[:, None] >= p[None, :])], axis=1))
    aug = np.zeros((128, 4, 16, 4, 2, 4), np.float32)
    for h in range(4):
        s = SLOPES[h]
        for ti in range(16):
            for u, sg in ((0, 1.0), (1, -1.0)):
                for gq in (0, 1):
                    aug[:, h, ti, gq, u, 0] = -sg * s * 128 * ti
                    aug[:, h, ti, gq, u, 1] = -sg * s * p
                    aug[:, h, ti, gq, u, 2] = 1.0
                    aug[:, h, ti, gq, u, 3] = 1.0
                for gk in (2, 3):
                    aug[:, h, ti, gk, u, 0] = 1.0
                    aug[:, h, ti, gk, u, 1] = 1.0
                    aug[:, h, ti, gk, u, 2] = sg * s * 128 * ti
                    aug[:, h, ti, gk, u, 3] = sg * s * p
    put("aug", aug)
    return cbt


def _pack_pf(inp):
    pft = np.zeros((128, PF_COLS), np.float32)
    p = np.arange(128)

    def put(name, arr):
        o, shp = PF_OFF[name]
        pft[:, o:o + int(np.prod(shp))] = np.asarray(arr, np.float32).reshape(128, -1)

    put("gattn", inp["attn_norm"].reshape(DEPTH, 8, 128).transpose(2, 0, 1))
    put("gffn", inp["ffn_norm"].reshape(DEPTH, 8, 128).transpose(2, 0, 1))
    gA = np.concatenate([inp["a_q_norm"]] * 3 + [inp["a_k_norm"]], axis=1)
    put("gA", np.broadcast_to(gA[None], (128, DEPTH, 256)))
    gB = np.concatenate([inp["b_q_norm"]] * 2 + [inp["b_k_norm"]] * 2, axis=1)
    put("gB", np.broadcast_to(gB[None], (128, DEPTH, 192)))
    bs = np.zeros((128, DEPTH, 4), np.float32)
    bs[0:96] = inp["b_sub_norm"].reshape(DEPTH, 4, 96).transpose(2, 0, 1)
    put("bsub", bs)
    put("blam", np.broadcast_to(inp["b_lambda"][None], (128, DEPTH, 4, 48)))
    put("clb", inp["c_lb_logits"].reshape(DEPTH, 2, 128).transpose(2, 1, 0))
    put("gco", inp["c_out_norm"].reshape(DEPTH, 2, 128).transpose(2, 0, 1))
    put("convw", inp["conv_w"].reshape(DEPTH, 3, NJ, 128).transpose(3, 0, 2, 1))
    put("convb", inp["conv_b"].reshape(DEPTH, NJ, 128).transpose(2, 0, 1))
    tok = (np.arange(16)[None, :] * 128 + p[:, None])
    row, col = tok // 64, tok % 64
    inv = (10000.0 ** (-np.arange(0, 32, 2, dtype=np.float32) / 32)).astype(np.float32)
    ang = np.stack([row, col], axis=-1).astype(np.float32)[..., None] * inv
    put("cos", np.cos(ang).astype(np.float32))
    put("sin", np.sin(ang).astype(np.float32))
    sel = np.zeros((128, 96), np.float32)
    sel[96, :] = 1.0
    put("sel", sel)
    ev = np.zeros((128, 4), np.float32)
    ev[:, 0], ev[:, 1], ev[:, 2] = EPS, 64 * EPS, 48 * EPS
    put("epsv", ev)
    return pft


def _pack_weights(inp):
    w_in = inp["w_in"]

    def tile_cols(cols):
        w = w_in[:, :, cols]
        return np.ascontiguousarray(w.reshape(DEPTH, 8, 128, len(cols)).transpose(0, 2, 1, 3))

    wA = np.stack([tile_cols(np.r_[192 * g:192 * g + 192, 384 + 64 * g:384 + 64 * g + 64, 512 + 64 * g:512 + 64 * g + 64]) for g in range(2)], axis=1)
    wB = np.stack([tile_cols(np.r_[640 + 96 * h:640 + 96 * h + 96, 1024 + 96 * h:1024 + 96 * h + 96, 1408 + 96 * h:1408 + 96 * h + 96]) for h in range(4)], axis=1)
    wC = np.stack([tile_cols(np.r_[1792 + 128 * q:1792 + 128 * q + 128, 2048 + 128 * q:2048 + 128 * q + 128, 2304 + 128 * q:2304 + 128 * q + 128,
                                   2816 + 128 * q:2816 + 128 * q + 128, 2560 + 128 * q:2560 + 128 * q + 128]) for q in range(2)], axis=1)
    w_out = inp["w_out"]
    wO = np.zeros((DEPTH, 9, 128, 1024), np.float32)
    for c in range(3):
        wO[:, c] = w_out[:, 128 * c:128 * c + 128]
    for h in range(4):
        wO[:, 3 + h, 0:96] = w_out[:, 384 + 96 * h:384 + 96 * h + 96]
    for q in range(2):
        wO[:, 7 + q] = w_out[:, 768 + 128 * q:768 + 128 * q + 128]
    w_up = inp["w_up"]
    wg = w_up[:, :, :DFF].reshape(DEPTH, 8, 128, NJ, 128)
    wv = w_up[:, :, DFF:].reshape(DEPTH, 8, 128, NJ, 128)
    wU = np.ascontiguousarray(np.concatenate([wg, wv], axis=-1).transpose(0, 3, 2, 1, 4))
    wD = np.ascontiguousarray(inp["w_down"].reshape(DEPTH, NJ, 128, 8, 128).transpose(0, 3, 2, 1, 4))
    return dict(wA=wA, wB=wB, wC=wC, wO=wO, wU=wU, wD=wD)


_CACHE = {}


def kernel(**inputs):
    inp = {k: np.asarray(v, np.float32) for k, v in inputs.items()}
    x = inp["x"]
    shared = _pack_weights(inp)
    shared["pF"] = _pack_pf(inp)
    shared["cB"] = _const_tables()
    if "nc" not in _CACHE:
        _CACHE["nc"] = build_program()
    nc = _CACHE["nc"]
    in_maps = []
    for c in range(NCORES):
        xs = x[c * NSEQ:(c + 1) * NSEQ]
        xT = np.ascontiguousarray(xs.transpose(0, 2, 1)).reshape(NSEQ, 8, 128, S)
        m = dict(shared)
        m["xT"] = xT
        in_maps.append(m)
    res = run_bass_kernel_spmd(nc, in_maps, core_ids=list(range(NCORES)))
    outs = []
    for c in range(NCORES):
        o = np.asarray(res.results[c]["outT"]).reshape(NSEQ, D, S)
        outs.append(o.transpose(0, 2, 1))
    return np.ascontiguousarray(np.concatenate(outs, axis=0)).astype(np.float32)
```
